# Optimizing a Trainium2 kernel written in Bass

```python
import math
import jax
import jax.numpy as jnp
from jax import lax
import numpy as np


D_MODEL = 1024
BATCH = 4
SEQ = 4096
DEPTH = 2

GRID_W = 64
CTX_LEN = 256
Q_BLOCK = 128
ROPE_THETA = 10000.0
NORM_EPS = 1e-6

MLA_HEADS = 8
MLA_Q_RANK = 384
MLA_KV_RANK = 256
MLA_NOPE = 64
MLA_ROPE = 32
MLA_V = 64
MLA_QK = MLA_NOPE + MLA_ROPE

SSD_HEADS = 16
SSD_HEAD_DIM = 64
SSD_INNER = SSD_HEADS * SSD_HEAD_DIM
SSD_GROUPS = 2
SSD_STATE = 128
SSD_CONV = 3
SSD_CHUNK = 128
SSD_CONV_CH = SSD_INNER + 2 * SSD_GROUPS * SSD_STATE

GQA_HEADS = 8
GQA_KV_HEADS = 2
GQA_HEAD_DIM = 64

N_BRANCH = 3
MLA_IN = MLA_Q_RANK + MLA_KV_RANK + MLA_ROPE
SSD_IN = SSD_INNER + SSD_CONV_CH + 2 * SSD_HEADS
GQA_IN = (GQA_HEADS + 2 * GQA_KV_HEADS) * GQA_HEAD_DIM
GATE_IN = N_BRANCH * D_MODEL
IN_WIDTH = MLA_IN + SSD_IN + GQA_IN + GATE_IN
MLA_OUT = MLA_HEADS * MLA_V
GQA_OUT = GQA_HEADS * GQA_HEAD_DIM
OUT_WIDTH = MLA_OUT + SSD_INNER + GQA_OUT

FFN_HIDDEN = -(-8 * D_MODEL // (3 * 256)) * 256
DEEPNORM_ALPHA = (2 * DEPTH) ** 0.25
DEEPNORM_BETA = (8 * DEPTH) ** -0.25

kernel_name = 'hybrid_mla_ssd_gqa_diffusion_block'


def layer_norm(x):
    xf = x.astype(jnp.float32)
    mu = jnp.mean(xf, axis=-1, keepdims=True)
    var = jnp.mean(jnp.square(xf - mu), axis=-1, keepdims=True)
    return ((xf - mu) * lax.rsqrt(var + NORM_EPS)).astype(x.dtype)


def layer_norm_affine(x, g, b):
    return layer_norm(x) * g + b


def rms_norm(x, g):
    xf = x.astype(jnp.float32)
    y = xf * lax.rsqrt(jnp.mean(jnp.square(xf), axis=-1, keepdims=True) + NORM_EPS)
    return (y * g).astype(x.dtype)


def axial_rope_tables(n_tokens, rot_dim):
    rows = n_tokens // GRID_W
    row = jnp.repeat(jnp.arange(rows), GRID_W)
    col = jnp.tile(jnp.arange(GRID_W), rows)
    n_freq = rot_dim // 4
    inv_freq = ROPE_THETA ** (-jnp.arange(n_freq, dtype=jnp.float32) / n_freq)
    ang = jnp.concatenate([row[:, None] * inv_freq, col[:, None] * inv_freq], axis=-1)
    return jnp.cos(ang), jnp.sin(ang)


def apply_rope(x, cos, sin):
    x1, x2 = jnp.split(x, 2, axis=-1)
    cos = cos[None, :, None, :].astype(x.dtype)
    sin = sin[None, :, None, :].astype(x.dtype)
    return jnp.concatenate([x1 * cos - x2 * sin, x1 * sin + x2 * cos], axis=-1)


def block_attention(q, k, v, scale):
    b, lq, hk, r, d = q.shape
    dv = v.shape[-1]
    n_blk = lq // Q_BLOCK
    qb = jnp.moveaxis(q.reshape(b, n_blk, Q_BLOCK, hk, r, d), 1, 0)

    def one_block(qi):
        s = jnp.einsum('bqhrd,bkhd->bhrqk', qi, k).astype(jnp.float32) * scale
        p = jax.nn.softmax(s, axis=-1).astype(v.dtype)
        return jnp.einsum('bhrqk,bkhd->bqhrd', p, v)

    o = lax.map(one_block, qb)
    return jnp.moveaxis(o, 0, 1).reshape(b, lq, hk * r * dv)


def mla_queries(p, w_uq, g_q, rope):
    b, l, _ = p.shape
    q = (rms_norm(p[..., :MLA_Q_RANK], g_q) @ w_uq).reshape(b, l, MLA_HEADS, MLA_QK)
    q_nope, q_rope = jnp.split(q, [MLA_NOPE], axis=-1)
    if rope is not None:
        q_rope = apply_rope(q_rope, *rope)
    return jnp.concatenate([q_nope, q_rope], axis=-1)[:, :, :, None, :]


def mla_keys_values(p, w_ukv, g_kv, rope):
    b, l, _ = p.shape
    ckv = p[..., MLA_Q_RANK:MLA_Q_RANK + MLA_KV_RANK]
    k_rope = p[..., MLA_Q_RANK + MLA_KV_RANK:][:, :, None, :]
    kv = (rms_norm(ckv, g_kv) @ w_ukv).reshape(b, l, MLA_HEADS, MLA_NOPE + MLA_V)
    k_nope, v = jnp.split(kv, [MLA_NOPE], axis=-1)
    if rope is not None:
        k_rope = apply_rope(k_rope, *rope)
    k = jnp.concatenate([k_nope, jnp.broadcast_to(k_rope, (b, l, MLA_HEADS, MLA_ROPE))], axis=-1)
    return k, v


def centred_depthwise_conv(u, w, bias):
    y = lax.conv_general_dilated(
        u, w[:, None, :].astype(u.dtype), window_strides=(1,),
        padding=[((SSD_CONV - 1) // 2, SSD_CONV // 2)],
        dimension_numbers=('NWC', 'WIO', 'NWC'), feature_group_count=u.shape[-1])
    return y + bias


def ssd_prepare(p, conv_w, conv_b, dt_bias):
    b, l, _ = p.shape
    z, xbc, dt = jnp.split(p, [SSD_INNER, SSD_INNER + SSD_CONV_CH], axis=-1)
    xbc = jax.nn.silu(centred_depthwise_conv(xbc, conv_w, conv_b))
    xs, bm, cm = jnp.split(xbc, [SSD_INNER, SSD_INNER + SSD_GROUPS * SSD_STATE], axis=-1)
    xs = xs.reshape(b, l, SSD_HEADS, SSD_HEAD_DIM)
    bm = bm.reshape(b, l, SSD_GROUPS, SSD_STATE)
    cm = cm.reshape(b, l, SSD_GROUPS, SSD_STATE)
    dt = jax.nn.softplus(dt.astype(jnp.float32) + dt_bias.reshape(-1).astype(jnp.float32))
    return z, xs, bm, cm, dt[..., :SSD_HEADS], dt[..., SSD_HEADS:]


def ssd_scan(x, dt, a, bm, cm, h0, return_y):
    b, l, h, p = x.shape
    g, n = bm.shape[-2:]
    r = h // g
    nc = l // SSD_CHUNK
    f32 = jnp.float32
    xdt = (x.astype(f32) * dt[..., None]).reshape(b, nc, SSD_CHUNK, g, r, p)
    bc = bm.astype(f32).reshape(b, nc, SSD_CHUNK, g, n)
    a_cum = jnp.cumsum((dt * a).reshape(b, nc, SSD_CHUNK, g, r), axis=2)
    decay_to_end = jnp.exp(a_cum[:, :, -1:] - a_cum)
    chunk_states = jnp.einsum('bckgn,bckgr,bckgrp->bcgrpn', bc, decay_to_end, xdt)
    chunk_decay = jnp.exp(a_cum[:, :, -1])

    def step(h_prev, inp):
        s, d = inp
        h_new = d[..., None, None] * h_prev + s
        return h_new, (h_prev if return_y else None)

    h_last, h_in = lax.scan(step, h0, (jnp.moveaxis(chunk_states, 1, 0), jnp.moveaxis(chunk_decay, 1, 0)))
    if not return_y:
        return None, h_last
    cc = cm.astype(f32).reshape(b, nc, SSD_CHUNK, g, n)
    seg = a_cum[:, :, :, None] - a_cum[:, :, None]
    in_order = jnp.tril(jnp.ones((SSD_CHUNK, SSD_CHUNK), bool))[None, None, :, :, None, None]
    decay = jnp.exp(jnp.where(in_order, seg, -jnp.inf))
    scores = jnp.einsum('bcqgn,bckgn->bcqkg', cc, bc)
    y_diag = jnp.einsum('bcqkg,bcqkgr,bckgrp->bcqgrp', scores, decay, xdt)
    y_off = jnp.einsum('bcqgn,bcgrpn,bcqgr->bcqgrp', cc, jnp.moveaxis(h_in, 0, 1), jnp.exp(a_cum))
    return (y_diag + y_off).reshape(b, l, h, p).astype(x.dtype), h_last


def ssd_branch(px, pc, conv_w, conv_b, a_log, dt_bias, d_skip, g_ssd, with_ctx):
    zx, xx, bx, cx, dtx_f, dtx_b = ssd_prepare(px, conv_w, conv_b, dt_bias)
    zc, xc, bc, cc, dtc_f, dtc_b = ssd_prepare(pc, conv_w, conv_b, dt_bias)
    a = -jnp.exp(a_log.astype(jnp.float32))
    b = px.shape[0]
    h0 = jnp.zeros((b, SSD_GROUPS, SSD_HEADS // SSD_GROUPS, SSD_HEAD_DIM, SSD_STATE), jnp.float32)
    fl = lambda t: jnp.flip(t, axis=1)
    yc_f, hc_f = ssd_scan(xc, dtc_f, a[0], bc, cc, h0, with_ctx)
    yc_b, hc_b = ssd_scan(fl(xc), fl(dtc_b), a[1], fl(bc), fl(cc), h0, with_ctx)
    yx_f, _ = ssd_scan(xx, dtx_f, a[0], bx, cx, hc_f, True)
    yx_b, _ = ssd_scan(fl(xx), fl(dtx_b), a[1], fl(bx), fl(cx), hc_b, True)

    def finish(y_f, y_b_rev, xs, z):
        y = y_f + fl(y_b_rev) + d_skip[:, None].astype(xs.dtype) * xs
        return rms_norm(y.reshape(z.shape) * jax.nn.silu(z), g_ssd)

    out_x = finish(yx_f, yx_b, xx, zx)
    out_c = finish(yc_f, yc_b, xc, zc) if with_ctx else None
    return out_x, out_c


def gqa_queries(p, g_q, rope):
    b, l, _ = p.shape
    q = rms_norm(p[..., :GQA_OUT].reshape(b, l, GQA_HEADS, GQA_HEAD_DIM), g_q)
    if rope is not None:
        q = apply_rope(q, *rope)
    return q.reshape(b, l, GQA_KV_HEADS, GQA_HEADS // GQA_KV_HEADS, GQA_HEAD_DIM)


def gqa_keys_values(p, g_k, rope):
    b, l, _ = p.shape
    kv_w = GQA_KV_HEADS * GQA_HEAD_DIM
    k = rms_norm(p[..., GQA_OUT:GQA_OUT + kv_w].reshape(b, l, GQA_KV_HEADS, GQA_HEAD_DIM), g_k)
    v = p[..., GQA_OUT + kv_w:].reshape(b, l, GQA_KV_HEADS, GQA_HEAD_DIM)
    if rope is not None:
        k = apply_rope(k, *rope)
    return k, v


def merge_branches(y_mla, y_ssd, y_gqa, gate_logits, b_gate, w_out):
    g = jax.nn.sigmoid((gate_logits + b_gate).astype(jnp.float32)).astype(y_mla.dtype)
    g_mla, g_ssd, g_gqa = jnp.split(g, N_BRANCH, axis=-1)
    w_mla, w_ssd, w_gqa = jnp.split(w_out, [MLA_OUT, MLA_OUT + SSD_INNER], axis=0)
    return g_mla * (y_mla @ w_mla) + g_ssd * (y_ssd @ w_ssd) + g_gqa * (y_gqa @ w_gqa)


def hybrid_mixer(hx, hc, rope_mla, rope_gqa, w_in, b_gate, w_uq, g_q_mla, w_ukv, g_kv_mla,
                 conv_w, conv_b, a_log, dt_bias, d_skip, g_ssd, g_q_gqa, g_k_gqa, w_out, with_ctx):
    cuts = [MLA_IN, MLA_IN + SSD_IN, MLA_IN + SSD_IN + GQA_IN]
    px_mla, px_ssd, px_gqa, px_gate = jnp.split(hx @ w_in, cuts, axis=-1)
    pc_mla, pc_ssd, pc_gqa, pc_gate = jnp.split(hc @ w_in, cuts, axis=-1)
    mla_scale = MLA_QK ** -0.5
    gqa_scale = GQA_HEAD_DIM ** -0.5

    kx, vx = mla_keys_values(px_mla, w_ukv, g_kv_mla, rope_mla)
    kc, vc = mla_keys_values(pc_mla, w_ukv, g_kv_mla, None)
    yx_mla = block_attention(mla_queries(px_mla, w_uq, g_q_mla, rope_mla),
                             jnp.concatenate([kc, kx], axis=1), jnp.concatenate([vc, vx], axis=1), mla_scale)
    yx_ssd, yc_ssd = ssd_branch(px_ssd, pc_ssd, conv_w, conv_b, a_log, dt_bias, d_skip, g_ssd, with_ctx)
    kx_g, vx_g = gqa_keys_values(px_gqa, g_k_gqa, rope_gqa)
    kc_g, vc_g = gqa_keys_values(pc_gqa, g_k_gqa, None)
    yx_gqa = block_attention(gqa_queries(px_gqa, g_q_gqa, rope_gqa),
                             jnp.concatenate([kc_g, kx_g], axis=1), jnp.concatenate([vc_g, vx_g], axis=1), gqa_scale)

    mx = merge_branches(yx_mla, yx_ssd, yx_gqa, px_gate, b_gate, w_out)
    mc = None
    if with_ctx:
        yc_mla = block_attention(mla_queries(pc_mla, w_uq, g_q_mla, None), kc, vc, mla_scale)
        yc_gqa = block_attention(gqa_queries(pc_gqa, g_q_gqa, None), kc_g, vc_g, gqa_scale)
        mc = merge_branches(yc_mla, yc_ssd, yc_gqa, pc_gate, b_gate, w_out)
    return mx, mc


def swiglu(h, w_ffn_in, w_ffn_out):
    gate, up = jnp.split(h @ w_ffn_in, 2, axis=-1)
    return (jax.nn.silu(gate) * up) @ w_ffn_out


def setup_inputs(seed: int = 0) -> dict:
    key = jax.random.key(seed)
    k = jax.random.split(key, 27)
    f32 = jnp.float32
    L = DEPTH

    def nrm(i, shape, scale):
        return jax.random.normal(k[i], shape, f32) * scale

    def gain(i, shape):
        return 1.0 + nrm(i, shape, 0.02)

    dt0 = jnp.exp(jax.random.uniform(k[15], (L, 2, SSD_HEADS), f32, math.log(1e-3), math.log(1e-1)))
    return {
        'x': nrm(0, (BATCH, SEQ, D_MODEL), 1.0),
        'c': nrm(1, (BATCH, D_MODEL), 1.0),
        'ctx': nrm(2, (BATCH, CTX_LEN, D_MODEL), 1.0),
        'c_ctx': nrm(3, (D_MODEL,), 1.0),
        'w_mod': nrm(4, (L, D_MODEL, 6 * D_MODEL), 0.5 * D_MODEL ** -0.5),
        'b_mod': nrm(5, (L, 6 * D_MODEL), 0.01),
        'w_in': nrm(6, (L, D_MODEL, IN_WIDTH), D_MODEL ** -0.5),
        'b_gate': nrm(7, (L, GATE_IN), 0.01),
        'w_uq': nrm(8, (L, MLA_Q_RANK, MLA_HEADS * MLA_QK), MLA_Q_RANK ** -0.5),
        'g_q_mla': gain(9, (L, MLA_Q_RANK)),
        'w_ukv': nrm(10, (L, MLA_KV_RANK, MLA_HEADS * (MLA_NOPE + MLA_V)), MLA_KV_RANK ** -0.5),
        'g_kv_mla': gain(11, (L, MLA_KV_RANK)),
        'conv_w': nrm(12, (L, SSD_CONV, SSD_CONV_CH), SSD_CONV ** -0.5),
        'conv_b': nrm(13, (L, SSD_CONV_CH), 0.01),
        'a_log': jnp.log(jax.random.uniform(k[14], (L, 2, SSD_HEADS), f32, 1.0, 16.0)),
        'dt_bias': dt0 + jnp.log(-jnp.expm1(-dt0)),
        'd_skip': gain(16, (L, SSD_HEADS)),
        'g_ssd': gain(17, (L, SSD_INNER)),
        'g_q_gqa': gain(18, (L, GQA_HEAD_DIM)),
        'g_k_gqa': gain(19, (L, GQA_HEAD_DIM)),
        'w_out': nrm(20, (L, OUT_WIDTH, D_MODEL), DEEPNORM_BETA * OUT_WIDTH ** -0.5),
        'ln1_g': gain(21, (L, D_MODEL)),
        'ln1_b': nrm(22, (L, D_MODEL), 0.01),
        'w_ffn_in': nrm(23, (L, D_MODEL, 2 * FFN_HIDDEN), D_MODEL ** -0.5),
        'w_ffn_out': nrm(24, (L, FFN_HIDDEN, D_MODEL), DEEPNORM_BETA * FFN_HIDDEN ** -0.5),
        'ln2_g': gain(25, (L, D_MODEL)),
        'ln2_b': nrm(26, (L, D_MODEL), 0.01),
    }


def reference(x, c, ctx, c_ctx, w_mod, b_mod, w_in, b_gate, w_uq, g_q_mla, w_ukv, g_kv_mla,
              conv_w, conv_b, a_log, dt_bias, d_skip, g_ssd, g_q_gqa, g_k_gqa, w_out,
              ln1_g, ln1_b, w_ffn_in, w_ffn_out, ln2_g, ln2_b):
    n_lat = x.shape[1]
    rope_mla = axial_rope_tables(n_lat, MLA_ROPE)
    rope_gqa = axial_rope_tables(n_lat, GQA_HEAD_DIM)
    silu_c = jax.nn.silu(c)[:, None, :]
    silu_cc = jax.nn.silu(c_ctx)
    xc = ctx
    for i in range(DEPTH):
        with_ctx = i < DEPTH - 1
        sh1, sc1, g1, sh2, sc2, g2 = jnp.split(silu_c @ w_mod[i] + b_mod[i], 6, axis=-1)
        csh1, csc1, cg1, csh2, csc2, cg2 = jnp.split(silu_cc @ w_mod[i] + b_mod[i], 6, axis=-1)
        hx = layer_norm(x) * (1 + sc1) + sh1
        hc = layer_norm(xc) * (1 + csc1) + csh1
        mx, mc = hybrid_mixer(hx, hc, rope_mla, rope_gqa, w_in[i], b_gate[i], w_uq[i], g_q_mla[i],
                              w_ukv[i], g_kv_mla[i], conv_w[i], conv_b[i], a_log[i], dt_bias[i],
                              d_skip[i], g_ssd[i], g_q_gqa[i], g_k_gqa[i], w_out[i], with_ctx)
        x = layer_norm_affine(DEEPNORM_ALPHA * x + g1 * mx, ln1_g[i], ln1_b[i])
        fx = swiglu(layer_norm(x) * (1 + sc2) + sh2, w_ffn_in[i], w_ffn_out[i])
        x = layer_norm_affine(DEEPNORM_ALPHA * x + g2 * fx, ln2_g[i], ln2_b[i])
        if with_ctx:
            xc = layer_norm_affine(DEEPNORM_ALPHA * xc + cg1 * mc, ln1_g[i], ln1_b[i])
            fc = swiglu(layer_norm(xc) * (1 + csc2) + csh2, w_ffn_in[i], w_ffn_out[i])
            xc = layer_norm_affine(DEEPNORM_ALPHA * xc + cg2 * fc, ln2_g[i], ln2_b[i])
    return x
```

```python
import math
from contextlib import ExitStack
import numpy as np
import concourse.bass as bass
import concourse.mybir as mybir
from concourse.bass_utils import run_bass_kernel_spmd

F32 = mybir.dt.float32
BF16 = mybir.dt.bfloat16
AF = mybir.ActivationFunctionType
ALU = mybir.AluOpType
AX = mybir.AxisListType

ENGS = ("pe", "act", "dve", "pool", "sp")

D = 1024
GRID_W = 64
EPS = 1e-6
MLA_H, MLA_QR, MLA_KVR, MLA_NOPE, MLA_ROPE, MLA_V = 8, 384, 256, 64, 32, 64
MLA_QK = MLA_NOPE + MLA_ROPE
SSD_H, SSD_P, SSD_G, SSD_N = 16, 64, 2, 128
SSD_INNER = SSD_H * SSD_P
SSD_CONV_CH = SSD_INNER + 2 * SSD_G * SSD_N
GQA_H, GQA_KV, GQA_D = 8, 2, 64
MLA_IN = MLA_QR + MLA_KVR + MLA_ROPE
SSD_IN = SSD_INNER + SSD_CONV_CH + 2 * SSD_H
GQA_IN = (GQA_H + 2 * GQA_KV) * GQA_D
GATE_IN = 3 * D
IN_W = MLA_IN + SSD_IN + GQA_IN + GATE_IN
O_SSD = MLA_IN
O_Z = O_SSD
O_XBC = O_SSD + SSD_INNER
O_DT = O_XBC + SSD_CONV_CH
O_GQA = MLA_IN + SSD_IN
O_GATE = O_GQA + GQA_IN
FFN_H = 2816
OUT_W = 2048


class Trk:
    __slots__ = ("name", "lastw", "readers")

    def __init__(self, name=""):
        self.name = name
        self.lastw = None
        self.readers = {}


class Sched:
    def __init__(self, nc, n_dma_sems=14):
        self.nc = nc
        self.sems = {}
        self.count = {}
        for e in ENGS:
            self.sems[e] = nc.alloc_semaphore(name=f"prog_{e}")
            self.count[e] = 0
        self.dma_pool = {}
        self.dma_next = {}
        for q in ("sp", "pool", "act"):
            keys = []
            for i in range(n_dma_sems):
                k = f"d_{q}_{i}"
                self.sems[k] = nc.alloc_semaphore(name=k)
                self.count[k] = 0
                keys.append(k)
            self.dma_pool[q] = keys
            self.dma_next[q] = 0
        self.known = {e: {} for e in ENGS}
        self.ops = {e: [] for e in ENGS}

    def _need(self, e, reads, writes):
        need = {}
        for t in reads:
            if t.lastw is not None:
                k, v = t.lastw
                if need.get(k, 0) < v:
                    need[k] = v
        for t in writes:
            if t.lastw is not None:
                k, v = t.lastw
                if need.get(k, 0) < v:
                    need[k] = v
            for k, v in t.readers.items():
                if need.get(k, 0) < v:
                    need[k] = v
        out = []
        kn = self.known[e]
        for k, v in need.items():
            if k == "pe" and e == "pe":
                continue
            if kn.get(k, 0) < v:
                kn[k] = v
                out.append((k, v))
        return out

    def _emit_waits(self, e, waits):
        for k, v in waits:
            sem = self.sems[k]
            self.ops[e].append(lambda eng, sem=sem, v=v: eng.wait_ge(sem, v))

    def op(self, e, fn, reads=(), writes=()):
        self._emit_waits(e, self._need(e, reads, writes))
        self.count[e] += 1
        v = self.count[e]
        sem = self.sems[e]
        self.ops[e].append(lambda eng, fn=fn, sem=sem: fn(eng).then_inc(sem, 1))
        for t in reads:
            t.readers[e] = v
        for t in writes:
            t.lastw = (e, v)
            t.readers = {}

    def dma(self, q, out_ap, in_ap, reads=(), writes=()):
        pool = self.dma_pool[q]
        k = pool[self.dma_next[q] % len(pool)]
        self.dma_next[q] += 1
        waits = self._need(q, reads, writes)
        prev = self.count[k]
        if prev and self.known[q].get(k, 0) < prev:
            self.known[q][k] = prev
            waits.append((k, prev))
        self._emit_waits(q, waits)
        self.count[k] += 16
        v = self.count[k]
        sem = self.sems[k]
        self.ops[q].append(lambda eng, o=out_ap, i=in_ap, sem=sem:
                           eng.dma_start(out=o, in_=i).then_inc(sem, 16))
        for t in reads:
            t.readers[k] = v
        for t in writes:
            t.lastw = (k, v)
            t.readers = {}

    def barrier(self):
        for e in ENGS:
            waits = []
            for k, v in self.count.items():
                if v and self.known[e].get(k, 0) < v:
                    self.known[e][k] = v
                    waits.append((k, v))
            self._emit_waits(e, waits)

    def finalize(self):
        self.barrier()
        nc = self.nc
        ops = self.ops
        with nc.Block() as block:
            @block.tensor
            def _(eng):
                for f in ops["pe"]:
                    f(eng)

            @block.scalar
            def _(eng):
                for f in ops["act"]:
                    f(eng)

            @block.vector
            def _(eng):
                for f in ops["dve"]:
                    f(eng)

            @block.gpsimd
            def _(eng):
                for f in ops["pool"]:
                    f(eng)

            @block.sync
            def _(eng):
                for f in ops["sp"]:
                    f(eng)


class T:
    def __init__(self, t, name):
        self.t = t
        self.k = Trk(name)

    def __getitem__(self, idx):
        return self.t[idx]


def bc(ap, shape):
    return ap.to_broadcast(list(shape))


class Builder:
    def __init__(self, SEQ, CTX, DEPTH, alpha, debug=False):
        self.SEQ, self.CTX, self.DEPTH, self.alpha, self.debug = SEQ, CTX, DEPTH, alpha, debug
        self.NT = SEQ + CTX
        self.NTILE = self.NT // 128
        self.CT_TILES = CTX // 128
        nc = bass.Bass("TRN2", target_bir_lowering=False)
        self.nc = nc
        self.S = Sched(nc)
        self.stack = None
        self.uid = 0
        self.dq = 0
        self.banks = []
        for i in range(8):
            t = nc.alloc_psum_tensor(f"bank{i}", [128, 512], F32)
            self.banks.append(T(t, f"bank{i}"))
        self.bi = 0
        self.cur = None
        self.slot = None
        self.sbi = [0, 0, 0]
        self.nslot = 2
        self.bankset = None
        self.bsi = 0
        self.store_q = "sp"

    def record(self, fn, bankset):
        self.cur, self.bankset, self.bsi = [], bankset, 0
        fn()
        lst = self.cur
        self.cur, self.bankset = None, None
        return lst

    def merge_emit(self, lists, lead=None):
        idx = [0] * len(lists)
        total = sum(len(l) for l in lists)
        lead = lead or [1.0] * len(lists)
        for _ in range(total):
            best, bf = None, 1e9
            for k, l in enumerate(lists):
                if idx[k] < len(l):
                    f = idx[k] / len(l) * lead[k]
                    if f < bf:
                        best, bf = k, f
            kind, args = lists[best][idx[best]]
            idx[best] += 1
            (self.S.op if kind == "op" else self.S.dma)(*args)

    def dram_in(self, name, shape, dt=F32):
        return self.nc.dram_tensor(name, list(shape), dt, kind="ExternalInput").ap()

    def dram_scr(self, name, shape, dt):
        kind = "ExternalOutput" if self.debug else "Internal"
        return self.nc.dram_tensor(name, list(shape), dt, kind=kind).ap(), None

    def sb(self, name, shape, dt=F32):
        self.uid += 1
        t = self.stack.enter_context(self.nc.sbuf_tensor(f"{name}_{self.uid}", list(shape), dt))
        return T(t, name)

    def bank(self):
        if self.bankset is not None:
            b = self.banks[self.bankset[self.bsi % len(self.bankset)]]
            self.bsi += 1
            return b
        if self.slot is None:
            b = self.banks[self.bi % 6]
            self.bi += 1
        else:
            nb = 6 // self.nslot
            b = self.banks[self.slot * nb + self.sbi[self.slot] % nb]
            self.sbi[self.slot] += 1
        return b

    def run_streams(self, items, body, W=2, loads=None):
        it = iter(items)
        outer = self.cur
        st = {}
        self.nslot = W

        def emit_op(op):
            kind, args = op
            if outer is not None:
                outer.append(op)
            else:
                (self.S.op if kind == "op" else self.S.dma)(*args)

        def rec(sl, fn, *a):
            self.cur, self.slot = [], sl
            fn(*a)
            lst = self.cur
            self.cur, self.slot = outer, None
            return lst

        def begin(sl, x, par, preloaded):
            if loads is not None:
                if not preloaded:
                    for op in rec(sl, loads, x, sl, par):
                        emit_op(op)
                lst = rec(sl, body, x, sl, par)
            else:
                lst = rec(sl, body, x, sl)
            st[sl] = {"l": lst, "i": 0, "par": par, "pre": None}

        def step(sl):
            s_ = st[sl]
            if s_["i"] < len(s_["l"]):
                emit_op(s_["l"][s_["i"]])
                s_["i"] += 1
                if loads is not None and s_["pre"] is None and s_["i"] == max(1, len(s_["l"]) // 2):
                    x2 = next(it, None)
                    if x2 is not None:
                        for op in rec(sl, loads, x2, sl, 1 - s_["par"]):
                            emit_op(op)
                        s_["pre"] = (x2,)
                return True
            return False
        x0 = next(it, None)
        if x0 is None:
            return
        begin(0, x0, 0, False)
        for _ in range(len(st[0]["l"]) // 2):
            step(0)
        for sl in range(1, W):
            x = next(it, None)
            if x is not None:
                begin(sl, x, 0, False)
        while st:
            for sl in list(st):
                if not step(sl):
                    s_ = st.pop(sl)
                    if s_["pre"] is not None:
                        begin(sl, s_["pre"][0], 1 - s_["par"], True)
                    else:
                        x = next(it, None)
                        if x is not None:
                            begin(sl, x, 1 - s_["par"], False)
                    if sl in st:
                        step(sl)

    def run_pair(self, fa, fb):
        lists = []
        for sl, f in enumerate((fa, fb)):
            self.cur, self.slot = [], sl
            if f is not None:
                f()
            lists.append(self.cur)
            self.cur, self.slot = None, None
        ia = ib = 0
        la, lb = lists
        while ia < len(la) or ib < len(lb):
            if ia < len(la):
                kind, args = la[ia]
                ia += 1
                (self.S.op if kind == "op" else self.S.dma)(*args)
            if ib < len(lb):
                kind, args = lb[ib]
                ib += 1
                (self.S.op if kind == "op" else self.S.dma)(*args)

    def phase_begin(self):
        self.S.barrier()
        self.stack = ExitStack()

    def sub_begin(self):
        self.S.barrier()
        self._outer = self.stack
        self.stack = ExitStack()

    def sub_end(self):
        self.S.barrier()
        self.stack.close()
        self.stack = self._outer

    def phase_end(self):
        self.S.barrier()
        self.stack.close()
        self.stack = None

    def op(self, e, fn, reads=(), writes=()):
        args = (e, fn, [r.k if isinstance(r, T) else r for r in reads if r is not None],
                [w.k if isinstance(w, T) else w for w in writes if w is not None])
        if self.cur is not None:
            self.cur.append(("op", args))
        else:
            self.S.op(*args)

    def dma(self, q, out_ap, in_ap, reads=(), writes=()):
        args = (q, out_ap, in_ap, [r.k if isinstance(r, T) else r for r in reads if r is not None],
                [w.k if isinstance(w, T) else w for w in writes if w is not None])
        if self.cur is not None:
            self.cur.append(("dma", args))
        else:
            self.S.dma(*args)

    def load(self, out_ap, in_ap, reads=(), writes=()):
        self.dma("sp", out_ap, in_ap, reads, writes)

    def store(self, out_ap, in_ap, reads=(), writes=()):
        self.dma(self.store_q, out_ap, in_ap, reads, writes)

    def mm(self, bankT, out_ap, lhsT, rhs, start, stop, reads):
        self.op("pe", lambda e: e.matmul(out_ap, lhsT=lhsT, rhs=rhs, start=start, stop=stop,
                                         skip_group_check=True),
                reads=reads, writes=[bankT])

    def transp(self, bankT, out_ap, in_ap, ident_ap, reads):
        self.op("pe", lambda e: e.transpose(out=out_ap, in_=in_ap, identity=ident_ap),
                reads=reads, writes=[bankT])

    def act(self, out_ap, in_ap, func, reads, writes, scale=None, bias=None, accum=None):
        kw = {}
        if scale is not None:
            kw["scale"] = scale
        if bias is not None:
            kw["bias"] = bias
        if accum is not None:
            kw["accum_out"] = accum
        self.op("act", lambda e: e.activation(out=out_ap, in_=in_ap, func=func, **kw), reads, writes)

    def tt(self, eng, out_ap, in0, in1, op, reads, writes):
        self.op(eng, lambda e: e.tensor_tensor(out=out_ap, in0=in0, in1=in1, op=op), reads, writes)

    def ts(self, eng, out_ap, in0, s1, s2, op0, op1, reads, writes):
        if op1 is None:
            self.op(eng, lambda e: e.tensor_scalar(out=out_ap, in0=in0, scalar1=s1, scalar2=None, op0=op0),
                    reads, writes)
        else:
            self.op(eng, lambda e: e.tensor_scalar(out=out_ap, in0=in0, scalar1=s1, scalar2=s2,
                                                   op0=op0, op1=op1), reads, writes)

    def copy(self, eng, out_ap, in_ap, reads, writes):
        if eng == "act":
            self.op("act", lambda e: e.copy(out=out_ap, in_=in_ap), reads, writes)
        else:
            self.op(eng, lambda e: e.tensor_copy(out=out_ap, in_=in_ap), reads, writes)

    def setup_consts(self):
        nc = self.nc
        self.cstack = ExitStack()
        old = self.stack
        self.stack = self.cstack
        self.ident = self.sb("ident", [128, 128], BF16)
        self.identf = self.sb("identf", [128, 128], F32)
        self.UI = self.sb("UI", [128, 128], F32)
        self.LI = self.sb("LI", [128, 128], F32)
        self.SL = self.sb("SL", [128, 128], F32)
        self.SU = self.sb("SU", [128, 128], F32)
        self.ones = self.sb("ones", [128, 128], F32)
        self.onesb = self.sb("onesb", [128, 128], BF16)
        self.eps = self.sb("eps", [128, 1], F32)
        self.one1 = self.sb("one1", [128, 1], F32)
        self.mhalf = self.sb("mhalf", [128, 1], F32)

        def tri(Tt, pat, cm, cmp):
            self.op("pool", lambda e: e.memset(Tt[:], 1.0), [], [Tt])
            self.op("pool", lambda e: e.affine_select(out=Tt[:], in_=Tt[:], pattern=[[pat, 128]],
                                                      compare_op=cmp, fill=0.0, base=0,
                                                      channel_multiplier=cm), [Tt], [Tt])
        tri(self.identf, 1, -1, ALU.is_equal)
        tri(self.UI, 1, -1, ALU.is_ge)
        tri(self.LI, -1, 1, ALU.is_ge)
        tri(self.SL, -1, 1, ALU.is_gt)
        tri(self.SU, 1, -1, ALU.is_gt)
        self.op("pool", lambda e: e.memset(self.ones[:], 1.0), [], [self.ones])
        self.op("pool", lambda e: e.memset(self.onesb[:], 1.0), [], [self.onesb])
        self.op("pool", lambda e: e.memset(self.eps[:], EPS), [], [self.eps])
        self.op("pool", lambda e: e.memset(self.one1[:], 1.0), [], [self.one1])
        self.op("pool", lambda e: e.memset(self.mhalf[:], -0.5), [], [self.mhalf])
        self.copy("dve", self.ident[:], self.identf[:], [self.identf], [self.ident])
        self.stack = old

    def bcast_load(self, dst, src_row_ap, n):
        self.load(dst[:, 0:n], src_row_ap.partition_broadcast(128), [], [dst])

    def rstd_from(self, ssum, n, out_rstd, tmp):
        self.act(tmp, ssum, AF.Sqrt, [], [], scale=1.0 / n, bias=self.eps[:, 0:1])

    def layer_norm_tile(self, x, xT, w, out_ap, outT, tmp_pool):
        st = tmp_pool
        junk, s1, s2, mv = st["junk"], st["s1"], st["s2"], st["mv"]
        self.act(junk[:, 0:w], x, AF.Identity, [xT], [junk, s1], accum=s1[:, 0:1])
        self.act(junk[:, 0:w], x, AF.Square, [xT], [junk, s2], accum=s2[:, 0:1])
        self.ts("dve", mv[:, 0:1], s1[:, 0:1], 1.0 / w, None, ALU.mult, None, [s1], [mv])
        self.ts("dve", mv[:, 1:2], s2[:, 0:1], 1.0 / w, None, ALU.mult, None, [s2], [mv])
        self.tt("dve", mv[:, 2:3], mv[:, 0:1], mv[:, 0:1], ALU.mult, [mv], [mv])
        self.op("dve", lambda e: e.scalar_tensor_tensor(out=mv[:, 4:5], in0=mv[:, 1:2], scalar=EPS,
                                                        in1=mv[:, 2:3], op0=ALU.add, op1=ALU.subtract),
                [mv], [mv])
        self.tt("pool", mv[:, 5:6], mv[:, 4:5], self.mhalf[:, 0:1], ALU.pow, [mv, self.mhalf], [mv])
        self.op("dve", lambda e: e.scalar_tensor_tensor(out=mv[:, 6:7], in0=mv[:, 0:1], scalar=-1.0,
                                                        in1=mv[:, 5:6], op0=ALU.mult, op1=ALU.mult),
                [mv], [mv])
        self.act(out_ap, x, AF.Identity, [xT, mv], [outT], scale=mv[:, 5:6], bias=mv[:, 6:7])

    def ln_tmps(self):
        return {"junk": self.sb("lnjunk", [128, D], F32), "s1": self.sb("lns1", [128, 1], F32),
                "s2": self.sb("lns2", [128, 1], F32), "mv": self.sb("lnmv", [128, 8], F32)}


class Model(Builder):
    def declare(self):
        L, NT = self.DEPTH, self.NT
        di = self.dram_in
        self.x_in = di("x_in", [NT, D])
        self.c_lay = di("c_lay", [128, 8, 2])
        self.w_mod = di("w_mod", [L, D, 6 * D])
        self.b_mod = di("b_mod", [L, 6 * D])
        self.w_in = di("w_in", [L, D, IN_W])
        self.b_gate = di("b_gate", [L, GATE_IN])
        self.w_uq = di("w_uq", [L, MLA_QR, MLA_H * MLA_QK])
        self.g_q_mla = di("g_q_mla", [L, MLA_QR])
        self.w_ukv = di("w_ukv", [L, MLA_KVR, MLA_H * 128])
        self.g_kv_mla = di("g_kv_mla", [L, MLA_KVR])
        self.conv_w_l = di("conv_w_l", [L, 128, 12, 3])
        self.conv_b_l = di("conv_b_l", [L, 128, 12])
        self.a_log = di("a_log", [L, 32])
        self.dt_bias = di("dt_bias", [L, 32])
        self.d_skip = di("d_skip", [L, 16])
        self.g_ssd = di("g_ssd", [L, SSD_INNER])
        self.g_q_gqa = di("g_q_gqa", [L, 64])
        self.g_k_gqa = di("g_k_gqa", [L, 64])
        self.w_out = di("w_out", [L, OUT_W, D])
        self.ln1_g = di("ln1_g", [L, D])
        self.ln1_b = di("ln1_b", [L, D])
        self.w_ffn_in = di("w_ffn_in", [L, D, 2 * FFN_H])
        self.w_ffn_out = di("w_ffn_out", [L, FFN_H, D])
        self.ln2_g = di("ln2_g", [L, D])
        self.ln2_b = di("ln2_b", [L, D])
        self.rope_m = di("rope_m", [NT, 2, 16])
        self.rope_g = di("rope_g", [NT, 2, 32])
        self.out = self.nc.dram_tensor("out", [self.SEQ, D], F32, kind="ExternalOutput").ap()
        ds = self.dram_scr
        self.MOD, self.kMOD = ds("MOD", [L, 2, 6 * D], F32)
        self.P, self.kP = ds("P", [NT, IN_W], BF16)
        self.DT, self.kDT = ds("DT", [NT, 32], F32)
        self.CV, self.kCV = ds("CV", [12, 128, NT], BF16)
        self.QTm, self.kQTm = ds("QTm", [MLA_H, MLA_QK, NT], BF16)
        self.KTm, self.kKTm = ds("KTm", [MLA_H, MLA_QK, NT], BF16)
        self.Vm, self.kVm = ds("Vm", [NT, MLA_H, 64], BF16)
        self.QTg, self.kQTg = ds("QTg", [GQA_H, 64, NT], BF16)
        self.KTg, self.kKTg = ds("KTg", [GQA_KV, 64, NT], BF16)
        self.Vg, self.kVg = ds("Vg", [NT, GQA_KV, 64], BF16)
        self.YT, self.kYT = ds("YT", [16, 128, NT], BF16)
        self.YF, self.kYF = ds("YF", [NT, SSD_INNER], F32)
        self.X1, self.kX1 = ds("X1", [NT, D], F32)
        self.X2, self.kX2 = ds("X2", [NT, D], F32)
        self.HID, self.kHID = ds("HID", [FFN_H // 128, 128, NT], BF16)
        self.YS, self.kYS = ds("YS", [NT, SSD_INNER], BF16)

    def mod_setup(self, li):
        cl = self.sb("cl", [128, 8, 2], F32)
        sc = self.sb("sc", [128, 8, 2], F32)
        bm = self.sb("bm", [2, 6 * D], F32)
        res = self.sb("modres", [2, 6 * D], F32)
        wst = [self.sb("wmod_st", [128, 8, 512], F32) for _ in range(2)]

        def main():
            self.load(cl[:], self.c_lay, [], [cl])
            self.act(sc[:], cl[:], AF.Silu, [cl], [sc])
            self.load(bm[:], self.b_mod[li:li + 1, :].to_broadcast([2, 6 * D]), [], [bm])
            for j in range(12):
                w = wst[j % 2]
                self.load(w[:], self.w_mod[li, :, j * 512:(j + 1) * 512].rearrange("(kc p) n -> p kc n", p=128),
                          [], [w])
                b = self.bank()
                for kc in range(8):
                    self.mm(b, b[0:2, :], sc[:, kc, :], w[:, kc, :], kc == 0, kc == 7, [sc, w])
                self.tt("dve", res[:, j * 512:(j + 1) * 512], b[0:2, :], bm[:, j * 512:(j + 1) * 512], ALU.add,
                        [b, bm], [res, b])
            for c0 in (D, 4 * D):
                self.ts("dve", res[:, c0:c0 + D], res[:, c0:c0 + D], 1.0, None, ALU.add, None, [res], [res])
            self.store(self.MOD[li], res[:], [res], [])
        return main

    def ph_mod(self, li):
        self.phase_begin()
        self.mod_setup(li)()
        self.phase_end()

    def mod_rows(self, dst, which, col0):
        self.load(dst[:], self.MOD[self.mod_li, which:which + 1, col0:col0 + D].to_broadcast([128, D]), [], [dst])

    def ln_mod_T(self, xt, tmps, scb, shb, hn, dstT, dst_ap_fn):
        xn = tmps["xn"]
        self.layer_norm_tile(xt[:], xt, D, xn[:], xn, tmps)
        self.tt("dve", xn[:], xn[:], scb[:], ALU.mult, [xn, scb], [xn])
        self.tt("pool", hn[:], xn[:], shb[:], ALU.add, [xn, shb], [hn])
        b = self.bank()
        bv = b[:].bitcast(BF16)
        for kc in range(8):
            self.transp(b, bv[:, kc * 128:(kc + 1) * 128], hn[:, kc * 128:(kc + 1) * 128], self.ident[:],
                        [hn, self.ident])
        self.copy("act", dst_ap_fn(), bv[:, 0:1024].rearrange("p (a b) -> p a b", a=8), [b], [dstT, b])

    def ph_proj(self, li, src, ksrc):
        NT, NTILE, CTX = self.NT, self.NTILE, self.CTX
        self.phase_begin()
        hT = self.sb("hT", [128, 8, NT], BF16)
        self.sub_begin()
        TM = []
        for _ in range(3):
            tm = self.ln_tmps()
            tm["xn"] = self.sb("xn", [128, D], F32)
            TM.append(tm)
        scb = [self.sb("scb", [128, D], F32) for _ in range(2)]
        shb = [self.sb("shb", [128, D], F32) for _ in range(2)]
        for w in range(2):
            self.mod_rows(scb[w], w, 1 * D)
            self.mod_rows(shb[w], w, 0)
        xts = [self.sb("xt", [128, D], F32) for _ in range(3)]
        hns = [self.sb("hn", [128, D], BF16) for _ in range(3)]

        def ln_body(t, sl):
            xt, hn = xts[sl], hns[sl]
            w = 1 if t < self.CT_TILES else 0
            self.load(xt[:], src[t * 128:(t + 1) * 128, :], [ksrc], [xt])
            self.ln_mod_T(xt, TM[sl], scb[w], shb[w], hn, hT, lambda t=t: hT[:, :, t * 128:(t + 1) * 128])
        self.run_streams(range(NTILE), ln_body, 3)
        self.sub_end()
        self.sub_begin()
        blocks = []
        for (a, bnd) in ((0, O_XBC), (O_GQA, O_GATE), (O_GATE, IN_W)):
            c = a
            while c < bnd:
                blocks.append((c, min(c + 512, bnd), False))
                c += 512
        blocks.append((O_DT, O_DT + 32, True))
        wbf = [self.sb("win_bf", [128, 8, 512], BF16) for _ in range(3)]
        ev = [self.sb("pev", [128, 512], BF16) for _ in range(4)]
        evf = [self.sb("pevf", [128, 32], F32) for _ in range(2)]
        bgp = self.sb("bgp", [128, GATE_IN], F32)
        self.load(bgp[:], self.b_gate[li:li + 1, :].to_broadcast([128, GATE_IN]), [], [bgp])

        def wload(bi_):
            c0, c1, _ = blocks[bi_]
            wb = wbf[bi_ % 3]
            self.dma("pool", wb[:, :, 0:c1 - c0], self.w_in[li, :, c0:c1].rearrange("(kc p) n -> p kc n", p=128),
                     [], [wb])
        wload(0)
        wload(1)
        for bi_, (c0, c1, isdt) in enumerate(blocks):
            wd = c1 - c0
            wb = wbf[bi_ % 3]
            if bi_ + 2 < len(blocks):
                wload(bi_ + 2)
            for t in range(NTILE):
                b = self.bank()
                for kc in range(8):
                    self.mm(b, b[:, 0:wd], hT[:, kc, t * 128:(t + 1) * 128], wb[:, kc, 0:wd], kc == 0, kc == 7,
                            [hT, wb])
                if isdt:
                    e = evf[t % 2]
                    self.copy("dve", e[:, 0:wd], b[:, 0:wd], [b], [e, b])
                    self.store(self.DT[t * 128:(t + 1) * 128, :], e[:, 0:wd], [e], [self.kDT])
                else:
                    e = ev[t % 4]
                    z0, z1 = max(c0, O_Z), min(c1, O_Z + SSD_INNER)
                    if z0 < z1:
                        if z0 > c0:
                            self.copy("dve", e[:, 0:z0 - c0], b[:, 0:z0 - c0], [b], [e, b])
                        self.act(e[:, z0 - c0:z1 - c0], b[:, z0 - c0:z1 - c0], AF.Silu, [b], [e, b])
                        if z1 < c1:
                            self.copy("dve", e[:, z1 - c0:wd], b[:, z1 - c0:wd], [b], [e, b])
                    elif c0 >= O_GATE:
                        self.tt("dve", e[:, 0:wd], b[:, 0:wd], bgp[:, c0 - O_GATE:c1 - O_GATE], ALU.add, [b, bgp],
                                [e, b])
                    else:
                        self.copy("act" if t % 2 else "dve", e[:, 0:wd], b[:, 0:wd], [b], [e, b])
                    self.store(self.P[t * 128:(t + 1) * 128, c0:c1], e[:, 0:wd], [e], [self.kP])
        self.sub_end()
        self.sub_begin()
        wbf = [self.sb("win_bf2", [128, 8, 128], BF16) for _ in range(3)]

        def cwload(j):
            c0 = O_XBC + j * 128
            self.dma("pool", wbf[j % 3][:], self.w_in[li, :, c0:c0 + 128].rearrange("(kc p) n -> p kc n", p=128),
                     [], [wbf[j % 3]])
        cwload(0)
        cwload(1)
        cw = self.sb("cw", [128, 12, 3], F32)
        cb = self.sb("cb", [128, 12], F32)
        self.load(cw[:], self.conv_w_l[li], [], [cw])
        self.load(cb[:], self.conv_b_l[li], [], [cb])
        U = [self.sb("U", [128, NT + 3], F32) for _ in range(2)]
        A = [self.sb("A", [128, NT + 1], F32) for _ in range(2)]
        CVs = [self.sb("CVs", [128, NT + 1], BF16) for _ in range(2)]
        for u in U:
            self.op("pool", lambda e, u=u: e.memset(u[:], 0.0), [], [u])
        tblocks = [(0, CTX)] + [(CTX + i * 512, min(CTX + (i + 1) * 512, NT)) for i in range((self.SEQ + 511) // 512)]
        W = NT + 1
        for j in range(12):
            c0 = O_XBC + j * 128
            wb = wbf[j % 3]
            u, a, cv = U[j % 2], A[j % 2], CVs[j % 2]
            if j + 2 < 12:
                cwload(j + 2)
            for (t0, t1) in tblocks:
                n = t1 - t0
                b = self.bank()
                for kc in range(8):
                    self.mm(b, b[:, 0:n], wb[:, kc, 0:128], hT[:, kc, t0:t1], kc == 0, kc == 7, [hT, wb])
                off = 1 + t0 if t0 < CTX else 2 + t0
                self.copy("act" if (t0 // 512) % 2 else "dve", u[:, off:off + n], b[:, 0:n], [b], [u, b])
            self.act(a[:], u[:, 1:1 + W], AF.Identity, [u, cw, cb], [a], scale=cw[:, j, 1:2], bias=cb[:, j:j + 1])
            self.op("dve", lambda e, a=a, u=u, j=j: e.scalar_tensor_tensor(
                out=a[:], in0=u[:, 0:W], scalar=cw[:, j, 0:1], in1=a[:], op0=ALU.mult, op1=ALU.add),
                [u, cw, a], [a])
            self.op("dve", lambda e, a=a, u=u, j=j: e.scalar_tensor_tensor(
                out=a[:], in0=u[:, 2:2 + W], scalar=cw[:, j, 2:3], in1=a[:], op0=ALU.mult, op1=ALU.add),
                [u, cw, a], [a])
            self.act(cv[:], a[:], AF.Silu, [a], [cv])
            self.store(self.CV[j, :, 0:CTX], cv[:, 0:CTX], [cv], [self.kCV])
            self.store(self.CV[j, :, CTX:NT], cv[:, CTX + 1:NT + 1], [cv], [self.kCV])
        self.store_q = "sp"
        self.sub_end()
        self.phase_end()

    def rms_scale(self, ssum_ap, n, w, tmp, out, reads):
        self.ts("dve", tmp[:, 0:w], ssum_ap, 1.0 / n, EPS, ALU.mult, ALU.add, list(reads), [tmp])
        self.tt("pool", out[:, 0:w], tmp[:, 0:w], self.mhalf[:, 0:1].to_broadcast([128, w]), ALU.pow,
                [tmp, self.mhalf], [out])

    def rope(self, d1, d2, dstT, x1, x2, srcT, cos, sin, tabT, tmps, H, hf):
        cb_ = cos.unsqueeze(1).to_broadcast([128, H, hf])
        sb_ = sin.unsqueeze(1).to_broadcast([128, H, hf])
        ta, tb = tmps
        va = ta[:, 0:H * hf].rearrange("p (h f) -> p h f", h=H)
        vb = tb[:, 0:H * hf].rearrange("p (h f) -> p h f", h=H)
        self.tt("dve", va, x1, cb_, ALU.mult, [srcT, tabT], [ta])
        self.tt("pool", vb, x2, sb_, ALU.mult, [srcT, tabT], [tb])
        self.tt("dve", d1, va, vb, ALU.subtract, [ta, tb], [dstT])
        self.tt("dve", va, x1, sb_, ALU.mult, [srcT, tabT], [ta])
        self.tt("pool", vb, x2, cb_, ALU.mult, [srcT, tabT], [tb])
        self.tt("dve", d2, va, vb, ALU.add, [ta, tb], [dstT])

    def heads_T(self, src, srcT, H, d, dstD, t):
        b = self.bank()
        bv = b[:].bitcast(BF16)
        for h in range(H):
            self.transp(b, bv[0:d, h * 128:(h + 1) * 128], src[:, h, :], self.ident[:], [srcT, self.ident])
        o = self.hT_out[self.hT_i % 2]
        self.hT_i += 1
        self.copy("act", o[0:d, 0:H, :], bv[0:d, 0:H * 128].rearrange("p (h n) -> p h n", h=H), [b], [o, b])
        self.store(dstD[:, :, t * 128:(t + 1) * 128].rearrange("h d n -> d h n"), o[0:d, 0:H, :], [o], [])

    def ph_post(self, li):
        NT, NTILE = self.NT, self.NTILE
        self.phase_begin()
        wuq = self.sb("wuq", [128, 3, 768], BF16)
        wkv = self.sb("wkv", [128, 2, 1024], BF16)
        self.dma("pool", wuq[:], self.w_uq[li].rearrange("(kc p) n -> p kc n", p=128), [], [wuq])
        self.dma("pool", wkv[:], self.w_ukv[li].rearrange("(kc p) n -> p kc n", p=128), [], [wkv])
        gq = self.sb("gq", [128, MLA_QR], F32)
        gkv = self.sb("gkv", [128, MLA_KVR], F32)
        gqg = self.sb("gqg", [128, 64], F32)
        gkg = self.sb("gkg", [128, 64], F32)
        self.load(gq[:], self.g_q_mla[li:li + 1, :].to_broadcast([128, MLA_QR]), [], [gq])
        self.load(gkv[:], self.g_kv_mla[li:li + 1, :].to_broadcast([128, MLA_KVR]), [], [gkv])
        self.load(gqg[:], self.g_q_gqa[li:li + 1, :].to_broadcast([128, 64]), [], [gqg])
        self.load(gkg[:], self.g_k_gqa[li:li + 1, :].to_broadcast([128, 64]), [], [gkg])
        def mk():
            d_ = {}
            d_["hTo"] = [self.sb("hTo", [128, 8, 128], BF16) for _ in range(2)]
            d_["in"] = [(self.sb("pm", [128, MLA_IN], BF16), self.sb("pg", [128, GQA_IN], BF16),
                         self.sb("rm", [128, 2, 16], F32), self.sb("rg", [128, 2, 32], F32)) for _ in range(2)]
            d_["junk"] = self.sb("junk", [128, 512], F32)
            d_["ss"] = self.sb("ss", [128, 16], F32)
            d_["sq"] = self.sb("sqt", [128, 16], F32)
            d_["rs"] = self.sb("rs", [128, 16], F32)
            d_["qn"] = self.sb("qn", [128, MLA_QR], BF16)
            d_["qnT"] = self.sb("qnT", [128, 3, 128], BF16)
            d_["qs"] = self.sb("qs", [128, 8, 96], F32)
            d_["qf"] = self.sb("qf", [128, 8, 96], BF16)
            d_["kf"] = self.sb("kf", [128, 8, 96], BF16)
            d_["vt"] = self.sb("vt", [128, 8, 64], BF16)
            d_["kr"] = self.sb("kr", [128, 32], F32)
            d_["krb"] = self.sb("krb", [128, 32], BF16)
            d_["ta"] = self.sb("ta", [128, 256], F32)
            d_["tb"] = self.sb("tb", [128, 256], F32)
            d_["gq_f"] = self.sb("gq_f", [128, 8, 64], F32)
            d_["gq_b"] = self.sb("gq_b", [128, 8, 64], BF16)
            d_["gk_f"] = self.sb("gk_f", [128, 2, 64], F32)
            d_["gk_b"] = self.sb("gk_b", [128, 2, 64], BF16)
            return d_
        TS = [mk(), mk(), mk()]
        self.hT_i = 0
        def loads(t, sl, par):
            pm, pg, rm, rg = TS[sl]["in"][par]
            r0, r1 = t * 128, (t + 1) * 128
            self.load(pm[:], self.P[r0:r1, 0:MLA_IN], [], [pm])
            self.load(pg[:], self.P[r0:r1, O_GQA:O_GATE], [], [pg])
            self.load(rm[:], self.rope_m[r0:r1], [], [rm])
            self.load(rg[:], self.rope_g[r0:r1], [], [rg])

        def body(t, sl, par):
            d_ = TS[sl]
            self.hT_out = d_["hTo"]
            pm, pg, rm, rg = d_["in"][par]
            junk, ss, sq, rs, qn, qnT, qs, qf, kf, vt = (d_[k] for k in ("junk", "ss", "sq", "rs", "qn", "qnT",
                                                                        "qs", "qf", "kf", "vt"))
            kr, krb, ta, tb, gq_f, gq_b, gk_f, gk_b = (d_[k] for k in ("kr", "krb", "ta", "tb", "gq_f", "gq_b",
                                                                       "gk_f", "gk_b"))
            r0, r1 = t * 128, (t + 1) * 128
            self.act(junk[:, 0:MLA_QR], pm[:, 0:MLA_QR], AF.Square, [pm], [junk, ss], accum=ss[:, 0:1])
            self.act(junk[:, 0:MLA_KVR], pm[:, MLA_QR:MLA_QR + MLA_KVR], AF.Square, [pm], [junk, ss],
                     accum=ss[:, 1:2])
            self.rms_scale(ss[:, 0:1], MLA_QR, 1, sq, rs, [ss])
            self.op("dve", lambda e, pm=pm: e.scalar_tensor_tensor(
                out=qn[:], in0=pm[:, 0:MLA_QR], scalar=rs[:, 0:1], in1=gq[:], op0=ALU.mult, op1=ALU.mult),
                [pm, rs, gq], [qn])
            b = self.bank()
            bv = b[:].bitcast(BF16)
            for c in range(3):
                self.transp(b, bv[:, c * 128:(c + 1) * 128], qn[:, c * 128:(c + 1) * 128], self.ident[:],
                            [qn, self.ident])
            self.copy("act", qnT[:], bv[:, 0:384].rearrange("p (a b) -> p a b", a=3), [b], [qnT, b])
            for (h0, h1) in ((0, 5), (5, 8)):
                b = self.bank()
                wd = (h1 - h0) * 96
                for c in range(3):
                    self.mm(b, b[:, 0:wd], qnT[:, c, :], wuq[:, c, h0 * 96:h1 * 96], c == 0, c == 2, [qnT, wuq])
                self.copy("act", qs[:, h0:h1, :], b[:, 0:wd].rearrange("p (h f) -> p h f", f=96), [b], [qs, b])
            self.copy("pool", qf[:, :, 0:64], qs[:, :, 0:64], [qs], [qf])
            self.rope(qf[:, :, 64:80], qf[:, :, 80:96], qf, qs[:, :, 64:80], qs[:, :, 80:96], qs,
                      rm[:, 0, :], rm[:, 1, :], rm, (ta, tb), 8, 16)
            self.heads_T(qf, qf, 8, 96, self.QTm, t)
            self.rms_scale(ss[:, 1:2], MLA_KVR, 1, sq, rs, [ss])
            self.op("dve", lambda e, pm=pm: e.scalar_tensor_tensor(
                out=qn[:, 0:MLA_KVR], in0=pm[:, MLA_QR:MLA_QR + MLA_KVR], scalar=rs[:, 0:1], in1=gkv[:],
                op0=ALU.mult, op1=ALU.mult), [pm, rs, gkv], [qn])
            b = self.bank()
            bv = b[:].bitcast(BF16)
            for c in range(2):
                self.transp(b, bv[:, c * 128:(c + 1) * 128], qn[:, c * 128:(c + 1) * 128], self.ident[:],
                            [qn, self.ident])
            self.copy("act", qnT[:, 0:2, :], bv[:, 0:256].rearrange("p (a b) -> p a b", a=2), [b], [qnT, b])
            for hb in range(2):
                b = self.bank()
                for c in range(2):
                    self.mm(b, b[:, 0:512], qnT[:, c, :], wkv[:, c, hb * 512:(hb + 1) * 512], c == 0, c == 1,
                            [qnT, wkv])
                v4 = b[:, 0:512].rearrange("p (h f) -> p h f", f=128)
                self.copy("act", kf[:, hb * 4:(hb + 1) * 4, 0:64], v4[:, :, 0:64], [b], [kf, b])
                self.copy("dve", vt[:, hb * 4:(hb + 1) * 4, :], v4[:, :, 64:128], [b], [vt, b])
            self.copy("dve", kr[:], pm[:, MLA_QR + MLA_KVR:MLA_IN], [pm], [kr])
            self.rope(krb[:, 0:16].unsqueeze(1), krb[:, 16:32].unsqueeze(1), krb, kr[:, 0:16].unsqueeze(1),
                      kr[:, 16:32].unsqueeze(1), kr, rm[:, 0, :], rm[:, 1, :], rm, (ta, tb), 1, 16)
            self.copy("pool", kf[:, :, 64:96], krb[:].unsqueeze(1).to_broadcast([128, 8, 32]), [krb], [kf])
            self.heads_T(kf, kf, 8, 96, self.KTm, t)
            self.store(self.Vm[r0:r1], vt[:], [vt], [])
            for (nm, H, c0, gvec, xf, xb, dst) in (("q", 8, 0, gqg, gq_f, gq_b, self.QTg),
                                                    ("k", 2, 512, gkg, gk_f, gk_b, self.KTg)):
                src = pg[:, c0:c0 + H * 64].rearrange("p (h f) -> p h f", h=H)
                jv = junk[:, 0:H * 64].rearrange("p (h f) -> p h f", h=H)
                self.tt("dve", jv, src, src, ALU.mult, [pg], [junk])
                self.op("dve", lambda e, jv=jv, H=H: e.tensor_reduce(out=ss[:, 2:2 + H], in_=jv, axis=AX.X,
                                                                      op=ALU.add), [junk], [ss])
                self.rms_scale(ss[:, 2:2 + H], 64, H, sq, rs, [ss])
                self.tt("dve", xf[:], src, rs[:, 0:H].unsqueeze(2).to_broadcast([128, H, 64]), ALU.mult,
                        [pg, rs], [xf])
                self.tt("pool", xf[:], xf[:], gvec[:].unsqueeze(1).to_broadcast([128, H, 64]), ALU.mult,
                        [xf, gvec], [xf])
                self.rope(xb[:, :, 0:32], xb[:, :, 32:64], xb, xf[:, :, 0:32], xf[:, :, 32:64], xf,
                          rg[:, 0, :], rg[:, 1, :], rg, (ta, tb), H, 32)
                self.heads_T(xb, xb, H, 64, dst, t)
            self.store(self.Vg[r0:r1].rearrange("n h f -> n (h f)"), pg[:, 640:768], [pg], [])
        if li + 1 < self.DEPTH:
            fm = self.mod_setup(li + 1)
            lp = self.record(lambda: self.run_streams(range(NTILE), body, 3, loads=loads), None)
            lm = self.record(fm, [6, 7])
            self.merge_emit([lp, lm])
        else:
            self.run_streams(range(NTILE), body, 3, loads=loads)
        self.phase_end()

    def attn_setup(self, li, QT, KT, V, H, kv_of, d, scale, ychunk0, with_ctx):
        NT, NTILE, CTX = self.NT, self.NTILE, self.CTX
        KTs = [self.sb("KTs", [128, NT], BF16) for _ in range(2)]
        V1s = [self.sb("V1s", [128, NTILE, 128], BF16) for _ in range(2)]
        QTs = [self.sb("QTs", [128, 512], BF16) for _ in range(3)]
        PTs = [self.sb("PTs", [128, 512], BF16) for _ in range(6)]
        dlos = [self.sb("dlo", [64, 512], F32) for _ in range(2)]
        osbs = [self.sb("osb", [128, 512], F32) for _ in range(2)]
        yv = [self.sb("yv", [64, 512], BF16) for _ in range(2)]
        for v1 in V1s:
            self.op("pool", lambda e, v1=v1: e.memset(v1[:], 0.0), [], [v1])
            self.op("pool", lambda e, v1=v1: e.memset(v1[:, :, 64:128], 1.0), [v1], [v1])
        for t_ in KTs + QTs:
            self.op("pool", lambda e, t_=t_: e.memset(t_[:], 0.0), [], [t_])
        qblocks = [(CTX + i * 512, min(CTX + (i + 1) * 512, NT), NTILE) for i in range((self.SEQ + 511) // 512)]
        if with_ctx:
            qblocks = [(0, CTX, self.CT_TILES)] + qblocks
        blocks = [(h, q0, q1, nkt) for h in range(H) for (q0, q1, nkt) in qblocks]

        def load_head(h):
            kt_, v1 = KTs[h % 2], V1s[h % 2]
            self.load(kt_[0:d, :], KT[kv_of(h)], [], [kt_])
            vsrc = V[:, kv_of(h), :].rearrange("(kt p) f -> p kt f", p=128)
            for k0 in range(0, NTILE, 8):
                k1 = min(k0 + 8, NTILE)
                self.load(v1[:, k0:k1, 0:64], vsrc[:, k0:k1, :], [], [v1])

        def load_q(bi):
            h, q0, q1, nkt = blocks[bi]
            qt = QTs[bi % 3]
            self.load(qt[0:d, 0:q1 - q0], QT[h, :, q0:q1], [], [qt])

        def main():
          SK = 2
          pending = None
          load_head(0)
          load_q(0)
          for bi, (h, q0, q1, nkt) in enumerate(blocks):
              nq = q1 - q0
              kt_, v1 = KTs[h % 2], V1s[h % 2]
              qt = QTs[bi % 3]
              ob = self.banks[6]
              if bi + 1 < len(blocks):
                  if blocks[bi + 1][0] != h:
                      load_head(blocks[bi + 1][0])
                  load_q(bi + 1)
              sb_of = {}
              groups = [(k, min(k + 2, nkt)) for k in range(0, nkt, 2)]
              ng = len(groups)
              ocnt = [0]

              def S_(kt):
                  sbk = self.banks[kt % 4]
                  sb_of[kt] = sbk
                  self.mm(sbk, sbk[:, 0:nq], kt_[:, kt * 128:(kt + 1) * 128], qt[:, 0:nq], True, True,
                          [kt_, qt])

              def issue_S(g):
                  for kt in reversed(range(*groups[g])):
                      S_(kt)

              def issue_exp(g):
                  for kt in range(*groups[g]):
                      sbk = sb_of.pop(kt)
                      pt = PTs[kt % 6]
                      self.act(pt[:, 0:nq], sbk[:, 0:nq], AF.Exp, [sbk], [pt, sbk], scale=scale)

              def issue_O(g):
                  for kt in reversed(range(*groups[g])):
                      pt = PTs[kt % 6]
                      self.mm(ob, ob[:, 0:nq], v1[:, kt, :], pt[:, 0:nq], ocnt[0] == 0, ocnt[0] == nkt - 1,
                              [v1, pt])
                      ocnt[0] += 1

              issue_S(0)
              if pending is not None:
                  pending()
                  pending = None
              for g in range(ng):
                  if g + 1 < ng:
                      issue_S(g + 1)
                  issue_exp(g)
                  if g >= 1:
                      issue_O(g - 1)
              issue_O(ng - 1)
              dlo, osb, y = dlos[bi % 2], osbs[bi % 2], yv[bi % 2]
              self.copy("dve", osb[:, 0:nq], ob[:, 0:nq], [ob], [osb, ob])

              def epi(nq=nq, dlo=dlo, osb=osb, y=y, h=h, q0=q0, q1=q1):
                  self.load(dlo[0:64, 0:nq], osb[64:128, 0:nq], [osb], [dlo])
                  self.op("dve", lambda e: e.reciprocal(out=dlo[0:64, 0:nq], in_=dlo[0:64, 0:nq]), [dlo], [dlo])
                  self.tt("dve", y[:, 0:nq], osb[0:64, 0:nq], dlo[0:64, 0:nq], ALU.mult, [osb, dlo], [y])
                  self.store(self.YT[ychunk0 + h // 2, (h % 2) * 64:(h % 2) * 64 + 64, q0:q1], y[:, 0:nq], [y], [])
              pending = epi
          if pending is not None:
              pending()
        return main


    def ssd_setup(self, li, d):
        NT, NTILE, CT = self.NT, self.NTILE, self.CT_TILES
        if True:
            aneg = self.sb("aneg", [128, 32], F32)
            dtb = self.sb("dtb", [128, 32], F32)
            self.load(aneg[:], self.a_log[li:li + 1, :].to_broadcast([128, 32]), [], [aneg])
            self.act(aneg[:], aneg[:], AF.Exp, [aneg], [aneg])
            self.ts("dve", aneg[:], aneg[:], -1.0, None, ALU.mult, None, [aneg], [aneg])
            self.load(dtb[:], self.dt_bias[li:li + 1, :].to_broadcast([128, 32]), [], [dtb])
            if d == 1:
                dsk = self.sb("dsk", [128, 16], F32)
                gss = self.sb("gss", [128, SSD_INNER], F32)
                self.load(dsk[:], self.d_skip[li:li + 1, :].to_broadcast([128, 16]), [], [dsk])
                self.load(gss[:], self.g_ssd[li:li + 1, :].to_broadcast([128, SSD_INNER]), [], [gss])
                sz = self.sb("sz", [128, SSD_INNER], F32)
                fss = self.sb("fss", [128, 2], F32)
                frs = self.sb("frs", [128, 2], F32)

            Lseg, Rseg, Lac, Lde = ((self.SL, self.UI, self.UI, self.SL) if d == 0 else
                                    (self.SU, self.LI, self.LI, self.SU))
            dt_all = self.sb("dt_all", [128, NTILE, 16], F32)
            dtA_all = self.sb("dtA_all", [128, NTILE, 16], F32)
            E_all = self.sb("E_all", [128, NTILE, 48], F32)
            sdt_all = self.sb("sdt_all", [128, NTILE, 16], F32)
            self.sub_begin()
            dtr_all = self.sb("dtr_all", [128, NTILE, 32], F32)
            self.load(dtr_all[:], self.DT.rearrange("(t p) c -> p t c", p=128), [], [dtr_all])
            self.tt("dve", dt_all[:], dtr_all[:, :, d * 16:(d + 1) * 16],
                    dtb[:, d * 16:(d + 1) * 16].unsqueeze(1).to_broadcast([128, NTILE, 16]), ALU.add,
                    [dtr_all, dtb], [dt_all])
            self.act(dt_all[:], dt_all[:], AF.Exp, [dt_all], [dt_all])
            self.act(dt_all[:], dt_all[:], AF.Ln, [dt_all, self.one1], [dt_all], bias=self.one1[:, 0:1])
            self.tt("dve", dtA_all[:], dt_all[:],
                    aneg[:, d * 16:(d + 1) * 16].unsqueeze(1).to_broadcast([128, NTILE, 16]), ALU.mult,
                    [dt_all, aneg], [dtA_all])
            for t0 in range(0, NTILE, 32):
                t1 = min(t0 + 32, NTILE)
                nn = (t1 - t0) * 16
                for k_, L_ in enumerate((Lac, Lde, self.ones)):
                    bk = self.bank()
                    self.mm(bk, bk[:, 0:nn], L_[:], dtA_all[:, t0:t1, :], True, True, [L_, dtA_all])
                    self.act(E_all[:, t0:t1, k_ * 16:(k_ + 1) * 16],
                             bk[:, 0:nn].rearrange("p (t c) -> p t c", c=16), AF.Exp, [bk], [E_all, bk])
            self.tt("dve", sdt_all[:], dt_all[:], E_all[:, :, 16:32], ALU.mult, [dt_all, E_all], [sdt_all])
            self.sub_end()

            def mk():
                d_ = {}
                d_["cx"] = self.sb("cvx", [128, 8, 128], BF16)
                d_["bt"] = self.sb("bct", [128, 4, 128], BF16)
                d_["dr"] = self.sb("dtr", [128, 32], F32)
                d_["Bm"] = self.sb("Bm", [128, 2, 128], BF16)
                d_["dt"] = self.sb("dt", [128, 16], F32)
                d_["dtA"] = self.sb("dtA", [128, 16], F32)
                d_["E"] = self.sb("E", [128, 48], F32)
                d_["sdt"] = self.sb("sdt", [128, 16], F32)
                d_["xdt"] = self.sb("xdt", [128, SSD_INNER], BF16)
                d_["xdd"] = self.sb("xdd", [128, SSD_INNER], BF16)
                d_["msc"] = [self.sb("msc", [128, 128], F32) for _ in range(2)]
                d_["lhs"] = [self.sb("lhs", [128, 8, 128], F32) for _ in range(2)]
                d_["Es"] = [[self.sb("Es", [128, 4, 128], F32) for _ in range(2)] for _ in range(2)]
                d_["M"] = [self.sb("M", [128, 8, 128], BF16) for _ in range(2)]
                d_["yds"] = [self.sb("yds", [128, 512], F32) for _ in range(2)]
                d_["ytmp"] = self.sb("ytmp", [128, 512], F32)
                return d_
            PB = [mk(), mk()]
            XS = [self.sb("xs", [128, SSD_INNER], BF16) for _ in range(3)]
            YA = [self.sb("yacc", [128, SSD_INNER], F32) for _ in range(3)]
            if d == 1:
                YFb = [self.sb("yf", [128, SSD_INNER], F32) for _ in range(3)]
                ZT = [self.sb("zs", [128, SSD_INNER], BF16) for _ in range(3)]
                YNB = [self.sb("ynb", [128, SSD_INNER], BF16) for _ in range(2)]
            hT = [self.sb("hT", [128, 512], F32) for _ in range(2)]
            hTb = [self.sb("hTb", [128, 512], BF16) for _ in range(2)]
            for g in range(2):
                self.op("pool", lambda e, h_=hT[g]: e.memset(h_[:], 0.0), [], [hT[g]])
                self.op("pool", lambda e, h_=hTb[g]: e.memset(h_[:], 0.0), [], [hTb[g]])
            Lseg, Rseg, Lac, Lde = ((self.SL, self.UI, self.UI, self.SL) if d == 0 else
                                    (self.SU, self.LI, self.LI, self.SU))
            order = (list(range(NTILE)) if d == 0 else
                     list(range(CT - 1, -1, -1)) + list(range(NTILE - 1, CT - 1, -1)))

            def stage1(it, t, d=d):
                p_ = PB[it % 2]
                cx, bt, Bm, xdt, xdd = (p_[k] for k in ("cx", "bt", "Bm", "xdt", "xdd"))
                xs = XS[it % 3]
                r0, r1 = t * 128, (t + 1) * 128
                self.load(cx[:], self.CV[0:8, :, r0:r1].rearrange("j p n -> p j n"), [], [cx])
                self.load(bt[:], self.CV[8:12, :, r0:r1].rearrange("j p n -> p j n"), [], [bt])
                if d == 1:
                    self.load(YFb[it % 3][:], self.YF[r0:r1, :], [], [YFb[it % 3]])
                    self.load(ZT[it % 3][:], self.P[r0:r1, O_Z:O_Z + SSD_INNER], [], [ZT[it % 3]])
                b = self.bank()
                bv = b[:].bitcast(BF16)
                for j in range(8):
                    self.transp(b, bv[:, j * 128:(j + 1) * 128], cx[:, j, :], self.ident[:], [cx, self.ident])
                self.copy("dve", xs[:], bv[:, 0:1024], [b], [xs, b])
                b = self.bank()
                bv = b[:].bitcast(BF16)
                for g in range(2):
                    self.transp(b, bv[:, g * 128:(g + 1) * 128], bt[:, g, :], self.ident[:], [bt, self.ident])
                self.copy("dve", Bm[:], bv[:, 0:256].rearrange("p (g n) -> p g n", g=2), [b], [Bm, b])
                xs3 = xs[:].rearrange("p (h f) -> p h f", h=16)
                self.tt("dve", xdt[:].rearrange("p (h f) -> p h f", h=16), xs3,
                        dt_all[:, t, :].unsqueeze(2).to_broadcast([128, 16, 64]), ALU.mult, [xs, dt_all], [xdt])
                self.tt("pool", xdd[:].rearrange("p (h f) -> p h f", h=16), xs3,
                        sdt_all[:, t, :].unsqueeze(2).to_broadcast([128, 16, 64]), ALU.mult, [xs, sdt_all], [xdd])

            def stage1g(it, t, g):
                p_ = PB[it % 2]
                bt, xdt, msc = p_["bt"], p_["xdt"], p_["msc"][g]
                if True:
                    lh, Mg = p_["lhs"][g], p_["M"][g]
                    scb_ = self.bank()
                    self.mm(scb_, scb_[:, 0:128], bt[:, g, :], bt[:, 2 + g, :], True, True, [bt])
                    self.tt("dve", msc[:], scb_[:, 0:128], Rseg[:], ALU.mult, [scb_, Rseg], [msc, scb_])
                    self.tt("pool", lh[:], Lseg[:].unsqueeze(1).to_broadcast([128, 8, 128]),
                            dtA_all[:, t, g * 8:(g + 1) * 8].unsqueeze(2).to_broadcast([128, 8, 128]), ALU.mult,
                            [Lseg, dtA_all], [lh])
                    for half in range(2):
                        sg = self.bank()
                        es = p_["Es"][g][half]
                        for i in range(4):
                            self.mm(sg, sg[:, i * 128:(i + 1) * 128], lh[:, half * 4 + i, :], Rseg[:], True, True,
                                    [lh, Rseg])
                        self.act(es[:], sg[:, 0:512].rearrange("p (h q) -> p h q", h=4), AF.Exp, [sg], [es, sg])
                        self.tt("dve", Mg[:, half * 4:(half + 1) * 4, :], es[:],
                                msc[:].unsqueeze(1).to_broadcast([128, 4, 128]), ALU.mult, [es, msc], [Mg])
                    yd = self.bank()
                    for h in range(8):
                        c0 = (g * 8 + h) * 64
                        self.mm(yd, yd[:, h * 64:(h + 1) * 64], Mg[:, h, :], xdt[:, c0:c0 + 64], True, True,
                                [Mg, xdt])
                    self.copy("dve", p_["yds"][g][:], yd[:, 0:512], [yd], [p_["yds"][g], yd])

            def stage2(it, t, d=d):
                p_ = PB[it % 2]
                bt, Bm, E, xdd, yds, ytmp = (p_[k] for k in ("bt", "Bm", "E", "xdd", "yds", "ytmp"))
                ya = YA[it % 3]
                r0, r1 = t * 128, (t + 1) * 128
                for g in range(2):
                    yo = self.bank()
                    self.mm(yo, yo[:, 0:512], bt[:, 2 + g, :], hTb[g][:], True, True, [bt, hTb[g]])
                    self.tt("dve", ytmp[:].rearrange("p (h f) -> p h f", h=8),
                            yo[:, 0:512].rearrange("p (h f) -> p h f", h=8),
                            E_all[:, t, g * 8:(g + 1) * 8].unsqueeze(2).to_broadcast([128, 8, 64]), ALU.mult,
                            [yo, E_all], [ytmp, yo])
                    self.tt("pool", ya[:, g * 512:(g + 1) * 512], ytmp[:], yds[g][:], ALU.add,
                            [ytmp, yds[g]], [ya])
                    Sb = self.bank()
                    self.mm(Sb, Sb[:, 0:512], Bm[:, g, :], xdd[:, g * 512:(g + 1) * 512], True, True, [Bm, xdd])
                    h3 = hT[g][:].rearrange("p (h f) -> p h f", h=8)
                    self.tt("pool", h3, h3,
                            E_all[:, t, 32 + g * 8:32 + (g + 1) * 8].unsqueeze(2).to_broadcast([128, 8, 64]),
                            ALU.mult, [hT[g], E_all, hTb[g]], [hT[g]])
                    self.tt("dve", hT[g][:], hT[g][:], Sb[:, 0:512], ALU.add, [hT[g], Sb], [hT[g], Sb])
                    self.copy("pool", hTb[g][:], hT[g][:], [hT[g]], [hTb[g]])
                if d == 0:
                    self.store(self.YF[r0:r1, :], ya[:], [ya], [])

            def stage3(it, t):
                ya, xs, yf, zt, ynb = YA[it % 3], XS[it % 3], YFb[it % 3], ZT[it % 3], YNB[it % 2]
                r0, r1 = t * 128, (t + 1) * 128
                xs3 = xs[:].rearrange("p (h f) -> p h f", h=16)
                self.tt("dve", ya[:], ya[:], yf[:], ALU.add, [ya, yf], [ya])
                self.tt("pool", sz[:].rearrange("p (h f) -> p h f", h=16), xs3,
                        dsk[:].unsqueeze(2).to_broadcast([128, 16, 64]), ALU.mult, [xs, dsk], [sz])
                self.tt("dve", ya[:], ya[:], sz[:], ALU.add, [ya, sz], [ya])
                self.tt("dve", ya[:], ya[:], zt[:], ALU.mult, [ya, zt], [ya])
                self.act(sz[:], ya[:], AF.Square, [ya], [sz, fss], accum=fss[:, 0:1])
                self.act(frs[:, 1:2], fss[:, 0:1], AF.Ln, [fss, self.eps], [frs], scale=1.0 / SSD_INNER,
                         bias=self.eps[:, 0:1])
                self.act(frs[:, 0:1], frs[:, 1:2], AF.Exp, [frs], [frs], scale=-0.5)
                self.op("dve", lambda e, ya=ya, ynb=ynb: e.scalar_tensor_tensor(
                    out=ynb[:], in0=ya[:], scalar=frs[:, 0:1], in1=gss[:], op0=ALU.mult, op1=ALU.mult),
                    [ya, frs, gss], [ynb])
                self.store(self.YS[r0:r1, :], ynb[:], [ynb], [])

            def rec(f, bankset):
                save = (self.cur, self.bankset, self.bsi)
                self.cur, self.bankset, self.bsi = [], bankset, 0
                f()
                lst = self.cur
                self.cur, self.bankset, self.bsi = save
                return lst

            def main():
                n = len(order)
                def s1_lists(it):
                    lc = rec(lambda: stage1(it, order[it]), [4])
                    lg0 = rec(lambda: stage1g(it, order[it], 0), [4])
                    lg1 = rec(lambda: stage1g(it, order[it], 1), [7])
                    wpos = {}
                    for i_, (kind, args) in enumerate(lc):
                        for tr_ in args[-1]:
                            wpos[id(tr_)] = i_
                    delay = 0
                    for j_, (kind, args) in enumerate(lg1):
                        for tr_ in list(args[-2]) + list(args[-1]):
                            if id(tr_) in wpos:
                                delay = max(delay, wpos[id(tr_)] + 1 - j_)
                    return lc + lg0, lg1, delay
                la, lb, _ = s1_lists(0)
                self.cur.extend(la)
                self.cur.extend(lb)
                for it in range(n + (1 if d == 1 else 0)):
                    subs, delays = [], []
                    if it < n:
                        subs.append(rec(lambda: stage2(it, order[it]), [5]))
                        delays.append(0)
                    if it + 1 < n:
                        la, lb, nc_ = s1_lists(it + 1)
                        subs.append(la)
                        delays.append(0)
                        subs.append(lb)
                        delays.append(nc_)
                    if d == 1 and it >= 1:
                        subs.append(rec(lambda: stage3(it - 1, order[it - 1]), [5]))
                        delays.append(0)
                    idx = [0] * len(subs)
                    rnd = 0
                    while any(idx[k] < len(subs[k]) for k in range(len(subs))):
                        for k in range(len(subs)):
                            if idx[k] < len(subs[k]) and rnd >= delays[k]:
                                self.cur.append(subs[k][idx[k]])
                                idx[k] += 1
                        rnd += 1
            return main

    def ln_affine_store(self, pre, tm, gb, bb_, outt, dst_ap):
        xn = tm["xn"]
        self.layer_norm_tile(pre[:], pre, D, xn[:], xn, tm)
        self.tt("dve", xn[:], xn[:], gb[:], ALU.mult, [xn, gb], [xn])
        self.tt("pool", outt[:], xn[:], bb_[:], ALU.add, [xn, bb_], [outt])
        self.store(dst_ap, outt[:], [outt], [])

    def ph_merge(self, li, src, with_ctx):
        NT, NTILE, CT = self.NT, self.NTILE, self.CT_TILES
        self.phase_begin()
        wout = self.sb("wout", [128, 16, D], BF16)
        woutk = [Trk("woutk%d" % i) for i in range(8)]
        for i in range(8):
            self.dma("pool", wout[:, 2 * i:2 * i + 2, :],
                     self.w_out[li, i * 256:(i + 1) * 256, :].rearrange("(kc p) n -> p kc n", p=128),
                     [], [woutk[i]])
        g1 = [self.sb("g1b", [128, D], F32) for _ in range(2)]
        for w in range(2):
            self.mod_rows(g1[w], w, 2 * D)
        lg = self.sb("lg", [128, D], F32)
        lb = self.sb("lb", [128, D], F32)
        self.load(lg[:], self.ln1_g[li:li + 1, :].to_broadcast([128, D]), [], [lg])
        self.load(lb[:], self.ln1_b[li:li + 1, :].to_broadcast([128, D]), [], [lb])
        def mk():
            d_ = {}
            d_["tm"] = self.ln_tmps()
            d_["tm"]["xn"] = self.sb("xn", [128, D], F32)
            d_["in"] = [(self.sb("yTs", [128, 16, 128], BF16), self.sb("gl", [128, GATE_IN], BF16),
                         self.sb("res", [128, D], F32), self.sb("ys", [128, SSD_INNER], BF16)) for _ in range(2)]
            d_["gate"] = self.sb("gate", [128, GATE_IN], F32)
            d_["mx"] = self.sb("mx", [128, D], F32)
            d_["tmp"] = self.sb("mtmp", [128, 512], F32)
            d_["o"] = self.sb("x1o", [128, D], F32)
            return d_
        TS = [mk(), mk()]
        tiles = list(range(NTILE)) if with_ctx else list(range(CT, NTILE))

        def loads(t, sl, par):
            yT, gl, r, ys = TS[sl]["in"][par]
            r0, r1 = t * 128, (t + 1) * 128
            self.load(yT[:, 0:4, :], self.YT[0:4, :, r0:r1].rearrange("j p n -> p j n"), [], [yT])
            self.load(yT[:, 12:16, :], self.YT[12:16, :, r0:r1].rearrange("j p n -> p j n"), [], [yT])
            self.load(ys[:], self.YS[r0:r1, :], [], [ys])
            self.load(gl[:], self.P[r0:r1, O_GATE:IN_W], [], [gl])
            self.load(r[:], src[r0:r1, :], [], [r])

        def body(t, sl, par):
            d_ = TS[sl]
            yT, gl, r, ys = d_["in"][par]
            o, gate, mx, tmp, tm = (d_[k] for k in ("o", "gate", "mx", "tmp", "tm"))
            r0, r1 = t * 128, (t + 1) * 128
            w = 1 if t < CT else 0
            b = self.bank()
            bv = b[:].bitcast(BF16)
            for j in range(8):
                self.transp(b, bv[:, j * 128:(j + 1) * 128], ys[:, j * 128:(j + 1) * 128], self.ident[:],
                            [ys, self.ident])
            self.copy("act", yT[:, 4:12, :], bv[:, 0:1024].rearrange("p (j n) -> p j n", j=8), [b], [yT, b])
            self.act(gate[:], gl[:], AF.Sigmoid, [gl], [gate])
            for br, (k0, k1) in enumerate(((0, 4), (4, 12), (12, 16))):
                for cb in range(2):
                    b = self.bank()
                    for kc in range(k0, k1):
                        self.mm(b, b[:, 0:512], yT[:, kc, :], wout[:, kc, cb * 512:(cb + 1) * 512], kc == k0,
                                kc == k1 - 1, [yT, woutk[kc // 2]])
                    gs = gate[:, br * D + cb * 512:br * D + (cb + 1) * 512]
                    if br == 0:
                        self.tt("dve", mx[:, cb * 512:(cb + 1) * 512], gs, b[:, 0:512], ALU.mult, [gate, b], [mx, b])
                    else:
                        self.tt("dve", tmp[:], gs, b[:, 0:512], ALU.mult, [gate, b], [tmp, b])
                        self.tt("pool", mx[:, cb * 512:(cb + 1) * 512], mx[:, cb * 512:(cb + 1) * 512], tmp[:],
                                ALU.add, [mx, tmp], [mx])
            self.tt("dve", mx[:], mx[:], g1[w][:], ALU.mult, [mx, g1[w]], [mx])
            self.op("dve", lambda e, r=r, mx=mx: e.scalar_tensor_tensor(
                out=mx[:], in0=r[:], scalar=float(self.alpha), in1=mx[:], op0=ALU.mult, op1=ALU.add),
                [r, mx], [mx])
            self.ln_affine_store(mx, tm, lg, lb, o, self.X1[r0:r1, :])
        self.run_streams(tiles, body, 2, loads=loads)
        self.phase_end()

    def ph_ffn(self, li, with_ctx, last):
        NT, NTILE, CT = self.NT, self.NTILE, self.CT_TILES
        NJ = FFN_H // 128
        blocks = []
        if with_ctx:
            blocks.append((0, CT, 1))
        t = CT
        while t < NTILE:
            blocks.append((t, min(t + 4, NTILE), 0))
            t += 4
        self.phase_begin()
        wfi = self.sb("wfi", [128, 8, 2 * FFN_H], BF16)
        wfik = [Trk("wfik%d" % i) for i in range(2 * FFN_H // 512)]
        for i in range(2 * FFN_H // 512):
            self.dma("pool", wfi[:, :, i * 512:(i + 1) * 512],
                     self.w_ffn_in[li, :, i * 512:(i + 1) * 512].rearrange("(kc p) n -> p kc n", p=128),
                     [], [wfik[i]])
        TM = []
        for _ in range(2):
            tm = self.ln_tmps()
            tm["xn"] = self.sb("xn", [128, D], F32)
            TM.append(tm)
        xts = [self.sb("xt", [128, D], F32) for _ in range(2)]
        hns = [self.sb("hn", [128, D], BF16) for _ in range(2)]
        h2Ts = [self.sb("h2T", [128, 8, 512], BF16) for _ in range(2)]
        h2Tk = [[Trk("h2Tk%d_%d" % (a, i)) for i in range(4)] for a in range(2)]
        hids = [self.sb("hid", [128, NJ, 512], BF16) for _ in range(1)] * 2
        sg = [self.sb("sg", [128, 512], F32) for _ in range(2)]
        mods = {}
        for w_ in sorted(set(b_[2] for b_ in blocks)):
            mods[w_] = (self.sb("shb%d" % w_, [128, D], F32), self.sb("scb%d" % w_, [128, D], F32))
            self.mod_rows(mods[w_][0], w_, 3 * D)
            self.mod_rows(mods[w_][1], w_, 4 * D)

        def pro_block(bi_):
            tb0, tb1, w = blocks[bi_]
            nt = tb1 - tb0
            h2T, hk = h2Ts[bi_ % 2], h2Tk[bi_ % 2]
            shb_, scb_ = mods[w]

            def pro(i, sl):
                tt_ = tb0 + i
                tm, hn, xt = TM[sl], hns[sl], xts[sl]
                self.load(xt[:], self.X1[tt_ * 128:(tt_ + 1) * 128, :], [], [xt])
                xn = tm["xn"]
                self.layer_norm_tile(xt[:], xt, D, xn[:], xn, tm)
                self.tt("dve", xn[:], xn[:], scb_[:], ALU.mult, [xn, scb_], [xn])
                self.tt("pool", hn[:], xn[:], shb_[:], ALU.add, [xn, shb_], [hn])
                b = self.bank()
                bv = b[:].bitcast(BF16)
                for kc in range(8):
                    self.transp(b, bv[:, kc * 128:(kc + 1) * 128], hn[:, kc * 128:(kc + 1) * 128], self.ident[:],
                                [hn, self.ident])
                self.copy("act", h2T[:, :, i * 128:(i + 1) * 128], bv[:, 0:1024].rearrange("p (a b) -> p a b", a=8),
                          [b], [hk[i], b])
            self.run_streams(range(nt), pro, 2)

        def j_block(bi_):
            tb0, tb1, w = blocks[bi_]
            nt = tb1 - tb0
            n = nt * 128
            h2T, hk, hid = h2Ts[bi_ % 2], h2Tk[bi_ % 2], hids[bi_ % 2]
            for j in range(NJ):
                bg_, bu_ = self.bank(), self.bank()
                for kc in range(8):
                    self.mm(bg_, bg_[:, 0:n], wfi[:, kc, j * 128:(j + 1) * 128], h2T[:, kc, 0:n], kc == 0, kc == 7,
                            [wfik[(j * 128) // 512]] + hk[0:nt])
                for kc in range(8):
                    self.mm(bu_, bu_[:, 0:n], wfi[:, kc, FFN_H + j * 128:FFN_H + (j + 1) * 128], h2T[:, kc, 0:n],
                            kc == 0, kc == 7, [wfik[(FFN_H + j * 128) // 512]] + hk[0:nt])
                s_ = sg[j % 2]
                self.act(s_[:, 0:n], bg_[:, 0:n], AF.Silu, [bg_], [s_, bg_])
                self.tt("dve", hid[:, j, 0:n], s_[:, 0:n], bu_[:, 0:n], ALU.mult, [s_, bu_], [hid, bu_])
            r0 = tb0 * 128
            half = NJ // 2
            self.store(self.HID[0:half, :, r0:r0 + n].rearrange("j p n -> p j n"), hid[:, 0:half, 0:n], [hid], [])
            self.store(self.HID[half:NJ, :, r0:r0 + n].rearrange("j p n -> p j n"), hid[:, half:NJ, 0:n], [hid], [])

        self.merge_emit([self.record(lambda: pro_block(0), [4, 5])])
        for bi_ in range(len(blocks)):
            lj = self.record(lambda: j_block(bi_), [0, 1, 2, 3])
            lp = self.record(lambda: pro_block(bi_ + 1), [4, 5]) if bi_ + 1 < len(blocks) else []
            self.merge_emit([lj, lp] if lp else [lj])
        self.phase_end()
        self.phase_begin()
        wfo = self.sb("wfo", [128, NJ, D], BF16)
        wfok = [Trk("wfok%d" % i) for i in range(NJ // 2)]
        for i in range(NJ // 2):
            self.dma("pool", wfo[:, 2 * i:2 * i + 2, :],
                     self.w_ffn_out[li, i * 256:(i + 1) * 256, :].rearrange("(kc p) n -> p kc n", p=128),
                     [], [wfok[i]])
        lg = self.sb("lg", [128, D], F32)
        lb = self.sb("lb", [128, D], F32)
        self.load(lg[:], self.ln2_g[li:li + 1, :].to_broadcast([128, D]), [], [lg])
        self.load(lb[:], self.ln2_b[li:li + 1, :].to_broadcast([128, D]), [], [lb])
        g2 = [self.sb("g2b", [128, D], F32) for _ in range(2)]
        for w in range(2):
            self.mod_rows(g2[w], w, 5 * D)

        def mk():
            d_ = {}
            d_["tm"] = self.ln_tmps()
            d_["tm"]["xn"] = self.sb("xn", [128, D], F32)
            d_["in"] = [(self.sb("hd", [128, NJ, 128], BF16), self.sb("x1t", [128, D], F32)) for _ in range(2)]
            d_["pre"] = self.sb("pre", [128, D], F32)
            d_["o"] = self.sb("x2o", [128, D], F32)
            return d_
        TS = [mk(), mk(), mk()]
        tiles = list(range(NTILE)) if with_ctx else list(range(CT, NTILE))

        def eloads(tt_, sl, par):
            hd, x1 = TS[sl]["in"][par]
            r0, r1 = tt_ * 128, (tt_ + 1) * 128
            half = NJ // 2
            self.load(hd[:, 0:half, :], self.HID[0:half, :, r0:r1].rearrange("j p n -> p j n"), [], [hd])
            self.load(hd[:, half:NJ, :], self.HID[half:NJ, :, r0:r1].rearrange("j p n -> p j n"), [], [hd])
            self.load(x1[:], self.X1[r0:r1, :], [], [x1])

        def epi(tt_, sl, par):
            d_ = TS[sl]
            tm, pre, o = d_["tm"], d_["pre"], d_["o"]
            hd, x1 = d_["in"][par]
            w = 1 if tt_ < CT else 0
            r0, r1 = tt_ * 128, (tt_ + 1) * 128
            for cb in range(2):
                b = self.bank()
                for j in range(NJ):
                    self.mm(b, b[:, 0:512], hd[:, j, :], wfo[:, j, cb * 512:(cb + 1) * 512], j == 0, j == NJ - 1,
                            [hd, wfok[j // 2]])
                self.tt("dve", pre[:, cb * 512:(cb + 1) * 512], g2[w][:, cb * 512:(cb + 1) * 512], b[:, 0:512],
                        ALU.mult, [g2[w], b], [pre, b])
            self.op("dve", lambda e, x1=x1, pre=pre: e.scalar_tensor_tensor(
                out=pre[:], in0=x1[:], scalar=float(self.alpha), in1=pre[:], op0=ALU.mult, op1=ALU.add),
                [x1, pre], [pre])
            if last:
                dst = self.out[(tt_ - CT) * 128:(tt_ - CT + 1) * 128, :]
            else:
                dst = self.X2[r0:r1, :]
            self.ln_affine_store(pre, tm, lg, lb, o, dst)
        self.run_streams(tiles, epi, 3, loads=eloads)
        self.phase_end()

    def build(self, upto=99):
        self.declare()
        self.stack = ExitStack()
        self.setup_consts()
        self.stack = None
        for li in range(self.DEPTH):
            last = li == self.DEPTH - 1
            with_ctx = not last
            src = self.x_in if li == 0 else self.X2
            self.mod_li = li
            if li == 0:
                self.ph_mod(li)
            self.ph_proj(li, src, None)
            self.ph_post(li)
            for d_ in range(2):
                self.phase_begin()
                if d_ == 0:
                    fa = self.attn_setup(li, self.QTm, self.KTm, self.Vm, MLA_H, lambda h: h, MLA_QK,
                                         MLA_QK ** -0.5, 0, with_ctx)
                else:
                    fa = self.attn_setup(li, self.QTg, self.KTg, self.Vg, GQA_H, lambda h: h // 4, GQA_D,
                                         GQA_D ** -0.5, 12, with_ctx)
                fb = self.ssd_setup(li, d_)
                la = self.record(fa, [0, 1, 2])
                lb = self.record(fb, [4, 5])
                self.merge_emit([la, lb], lead=[1.0, 0.92])
                self.phase_end()
            self.ph_merge(li, src, with_ctx)
            self.ph_ffn(li, with_ctx, last)
        self.S.finalize()
        return self.nc


def rope_tables(seq, ctx, rot_dim):
    rows = seq // GRID_W
    row = np.repeat(np.arange(rows), GRID_W).astype(np.float32)
    col = np.tile(np.arange(GRID_W), rows).astype(np.float32)
    n_freq = rot_dim // 4
    inv_freq = (np.float32(10000.0) ** (-np.arange(n_freq, dtype=np.float32) / np.float32(n_freq))).astype(np.float32)
    ang = np.concatenate([row[:, None] * inv_freq, col[:, None] * inv_freq], axis=-1).astype(np.float32)
    tab = np.zeros((ctx + seq, 2, rot_dim // 2), np.float32)
    tab[:ctx, 0, :] = 1.0
    tab[ctx:, 0, :] = np.cos(ang)
    tab[ctx:, 1, :] = np.sin(ang)
    return tab


_CACHE = {}


def make_in_maps(inp, n_cores, SEQ, CTX, L):
    f = lambda a: np.ascontiguousarray(np.asarray(a, dtype=np.float32))
    B = inp["x"].shape[0]
    shared = {k: f(inp[k]) for k in ("w_mod", "b_mod", "w_in", "b_gate", "w_uq", "g_q_mla", "w_ukv", "g_kv_mla",
                                     "d_skip", "g_ssd", "g_q_gqa", "g_k_gqa", "w_out", "ln1_g", "ln1_b",
                                     "w_ffn_in", "w_ffn_out", "ln2_g", "ln2_b")}
    shared["a_log"] = f(inp["a_log"]).reshape(L, 32)
    shared["dt_bias"] = f(inp["dt_bias"]).reshape(L, 32)
    cw = f(inp["conv_w"])
    shared["conv_w_l"] = np.ascontiguousarray(cw.reshape(L, 3, 12, 128).transpose(0, 3, 2, 1))
    shared["conv_b_l"] = np.ascontiguousarray(f(inp["conv_b"]).reshape(L, 12, 128).transpose(0, 2, 1))
    shared["rope_m"] = rope_tables(SEQ, CTX, MLA_ROPE)
    shared["rope_g"] = rope_tables(SEQ, CTX, GQA_D)
    maps = []
    for c in range(n_cores):
        b = c % B
        m = dict(shared)
        m["x_in"] = np.ascontiguousarray(np.concatenate([f(inp["ctx"][b]), f(inp["x"][b])], axis=0))
        cl = np.stack([f(inp["c"][b]).reshape(8, 128).T, f(inp["c_ctx"]).reshape(8, 128).T], axis=-1)
        m["c_lay"] = np.ascontiguousarray(cl)
        maps.append(m)
    return maps


def kernel(**inputs):
    x = np.asarray(inputs["x"])
    B, SEQ, _ = x.shape
    CTX = np.asarray(inputs["ctx"]).shape[1]
    L = np.asarray(inputs["w_in"]).shape[0]
    alpha = (2 * L) ** 0.25
    n_cores = 8
    key = (SEQ, CTX, L)
    if key not in _CACHE:
        _CACHE[key] = Model(SEQ, CTX, L, alpha).build()
    nc = _CACHE[key]
    in_maps = make_in_maps(inputs, n_cores, SEQ, CTX, L)
    res = run_bass_kernel_spmd(nc, in_maps, core_ids=list(range(n_cores)))
    out = np.stack([np.asarray(res.results[b]["out"], dtype=np.float32) for b in range(B)], axis=0)
    return out
```

```python
import math
from contextlib import ExitStack
import numpy as np
import concourse.bass as bass
import concourse.mybir as mybir
from concourse.bass_utils import run_bass_kernel_spmd

F32 = mybir.dt.float32
BF16 = mybir.dt.bfloat16
AF = mybir.ActivationFunctionType
ALU = mybir.AluOpType
AX = mybir.AxisListType

ENGS = ("pe", "act", "dve", "pool", "sp")

D = 1024
GRID_W = 64
EPS = 1e-6
MLA_H, MLA_QR, MLA_KVR, MLA_NOPE, MLA_ROPE, MLA_V = 8, 384, 256, 64, 32, 64
MLA_QK = MLA_NOPE + MLA_ROPE
SSD_H, SSD_P, SSD_G, SSD_N = 16, 64, 2, 128
SSD_INNER = SSD_H * SSD_P
SSD_CONV_CH = SSD_INNER + 2 * SSD_G * SSD_N
GQA_H, GQA_KV, GQA_D = 8, 2, 64
MLA_IN = MLA_QR + MLA_KVR + MLA_ROPE
SSD_IN = SSD_INNER + SSD_CONV_CH + 2 * SSD_H
GQA_IN = (GQA_H + 2 * GQA_KV) * GQA_D
GATE_IN = 3 * D
IN_W = MLA_IN + SSD_IN + GQA_IN + GATE_IN
O_SSD = MLA_IN
O_Z = O_SSD
O_XBC = O_SSD + SSD_INNER
O_DT = O_XBC + SSD_CONV_CH
O_GQA = MLA_IN + SSD_IN
O_GATE = O_GQA + GQA_IN
FFN_H = 2816
OUT_W = 2048


class Trk:
    __slots__ = ("name", "lastw", "readers")

    def __init__(self, name=""):
        self.name = name
        self.lastw = None
        self.readers = {}


class Sched:
    def __init__(self, nc, n_dma_sems=14):
        self.nc = nc
        self.sems = {}
        self.count = {}
        for e in ENGS:
            self.sems[e] = nc.alloc_semaphore(name=f"prog_{e}")
            self.count[e] = 0
        self.dma_pool = {}
        self.dma_next = {}
        for q in ("sp", "pool", "act"):
            keys = []
            for i in range(n_dma_sems):
                k = f"d_{q}_{i}"
                self.sems[k] = nc.alloc_semaphore(name=k)
                self.count[k] = 0
                keys.append(k)
            self.dma_pool[q] = keys
            self.dma_next[q] = 0
        self.known = {e: {} for e in ENGS}
        self.ops = {e: [] for e in ENGS}

    def _need(self, e, reads, writes):
        need = {}
        for t in reads:
            if t.lastw is not None:
                k, v = t.lastw
                if need.get(k, 0) < v:
                    need[k] = v
        for t in writes:
            if t.lastw is not None:
                k, v = t.lastw
                if need.get(k, 0) < v:
                    need[k] = v
            for k, v in t.readers.items():
                if need.get(k, 0) < v:
                    need[k] = v
        out = []
        kn = self.known[e]
        for k, v in need.items():
            if k == "pe" and e == "pe":
                continue
            if kn.get(k, 0) < v:
                kn[k] = v
                out.append((k, v))
        return out

    def _emit_waits(self, e, waits):
        for k, v in waits:
            sem = self.sems[k]
            self.ops[e].append(lambda eng, sem=sem, v=v: eng.wait_ge(sem, v))

    def op(self, e, fn, reads=(), writes=()):
        self._emit_waits(e, self._need(e, reads, writes))
        self.count[e] += 1
        v = self.count[e]
        sem = self.sems[e]
        self.ops[e].append(lambda eng, fn=fn, sem=sem: fn(eng).then_inc(sem, 1))
        for t in reads:
            t.readers[e] = v
        for t in writes:
            t.lastw = (e, v)
            t.readers = {}

    def dma(self, q, out_ap, in_ap, reads=(), writes=()):
        pool = self.dma_pool[q]
        k = pool[self.dma_next[q] % len(pool)]
        self.dma_next[q] += 1
        waits = self._need(q, reads, writes)
        prev = self.count[k]
        if prev and self.known[q].get(k, 0) < prev:
            self.known[q][k] = prev
            waits.append((k, prev))
        self._emit_waits(q, waits)
        self.count[k] += 16
        v = self.count[k]
        sem = self.sems[k]
        self.ops[q].append(lambda eng, o=out_ap, i=in_ap, sem=sem:
                           eng.dma_start(out=o, in_=i).then_inc(sem, 16))
        for t in reads:
            t.readers[k] = v
        for t in writes:
            t.lastw = (k, v)
            t.readers = {}

    def barrier(self):
        for e in ENGS:
            waits = []
            for k, v in self.count.items():
                if v and self.known[e].get(k, 0) < v:
                    self.known[e][k] = v
                    waits.append((k, v))
            self._emit_waits(e, waits)

    def finalize(self):
        self.barrier()
        nc = self.nc
        ops = self.ops
        with nc.Block() as block:
            @block.tensor
            def _(eng):
                for f in ops["pe"]:
                    f(eng)

            @block.scalar
            def _(eng):
                for f in ops["act"]:
                    f(eng)

            @block.vector
            def _(eng):
                for f in ops["dve"]:
                    f(eng)

            @block.gpsimd
            def _(eng):
                for f in ops["pool"]:
                    f(eng)

            @block.sync
            def _(eng):
                for f in ops["sp"]:
                    f(eng)


class T:
    def __init__(self, t, name):
        self.t = t
        self.k = Trk(name)

    def __getitem__(self, idx):
        return self.t[idx]


def bc(ap, shape):
    return ap.to_broadcast(list(shape))


class Builder:
    def __init__(self, SEQ, CTX, DEPTH, alpha, debug=False):
        self.SEQ, self.CTX, self.DEPTH, self.alpha, self.debug = SEQ, CTX, DEPTH, alpha, debug
        self.NT = SEQ + CTX
        self.NTILE = self.NT // 128
        self.CT_TILES = CTX // 128
        nc = bass.Bass("TRN2", target_bir_lowering=False)
        self.nc = nc
        self.S = Sched(nc)
        self.stack = None
        self.uid = 0
        self.dq = 0
        self.banks = []
        for i in range(8):
            t = nc.alloc_psum_tensor(f"bank{i}", [128, 512], F32)
            self.banks.append(T(t, f"bank{i}"))
        self.bi = 0
        self.cur = None
        self.slot = None
        self.sbi = [0, 0, 0]
        self.nslot = 2
        self.bankset = None
        self.bsi = 0
        self.store_q = "sp"

    def record(self, fn, bankset):
        self.cur, self.bankset, self.bsi = [], bankset, 0
        fn()
        lst = self.cur
        self.cur, self.bankset = None, None
        return lst

    def merge_emit(self, lists):
        idx = [0] * len(lists)
        total = sum(len(l) for l in lists)
        for _ in range(total):
            best, bf = None, 2.0
            for k, l in enumerate(lists):
                if idx[k] < len(l):
                    f = idx[k] / len(l)
                    if f < bf:
                        best, bf = k, f
            kind, args = lists[best][idx[best]]
            idx[best] += 1
            (self.S.op if kind == "op" else self.S.dma)(*args)

    def dram_in(self, name, shape, dt=F32):
        return self.nc.dram_tensor(name, list(shape), dt, kind="ExternalInput").ap()

    def dram_scr(self, name, shape, dt):
        kind = "ExternalOutput" if self.debug else "Internal"
        return self.nc.dram_tensor(name, list(shape), dt, kind=kind).ap(), None

    def sb(self, name, shape, dt=F32):
        self.uid += 1
        t = self.stack.enter_context(self.nc.sbuf_tensor(f"{name}_{self.uid}", list(shape), dt))
        return T(t, name)

    def bank(self):
        if self.bankset is not None:
            b = self.banks[self.bankset[self.bsi % len(self.bankset)]]
            self.bsi += 1
            return b
        if self.slot is None:
            b = self.banks[self.bi % 6]
            self.bi += 1
        else:
            nb = 6 // self.nslot
            b = self.banks[self.slot * nb + self.sbi[self.slot] % nb]
            self.sbi[self.slot] += 1
        return b

    def run_streams(self, items, body, W=2, loads=None):
        it = iter(items)
        outer = self.cur
        st = {}
        self.nslot = W

        def emit_op(op):
            kind, args = op
            if outer is not None:
                outer.append(op)
            else:
                (self.S.op if kind == "op" else self.S.dma)(*args)

        def rec(sl, fn, *a):
            self.cur, self.slot = [], sl
            fn(*a)
            lst = self.cur
            self.cur, self.slot = outer, None
            return lst

        def begin(sl, x, par, preloaded):
            if loads is not None:
                if not preloaded:
                    for op in rec(sl, loads, x, sl, par):
                        emit_op(op)
                lst = rec(sl, body, x, sl, par)
            else:
                lst = rec(sl, body, x, sl)
            st[sl] = {"l": lst, "i": 0, "par": par, "pre": None}

        def step(sl):
            s_ = st[sl]
            if s_["i"] < len(s_["l"]):
                emit_op(s_["l"][s_["i"]])
                s_["i"] += 1
                if loads is not None and s_["pre"] is None and s_["i"] == max(1, len(s_["l"]) // 2):
                    x2 = next(it, None)
                    if x2 is not None:
                        for op in rec(sl, loads, x2, sl, 1 - s_["par"]):
                            emit_op(op)
                        s_["pre"] = (x2,)
                return True
            return False
        x0 = next(it, None)
        if x0 is None:
            return
        begin(0, x0, 0, False)
        for _ in range(len(st[0]["l"]) // 2):
            step(0)
        for sl in range(1, W):
            x = next(it, None)
            if x is not None:
                begin(sl, x, 0, False)
        while st:
            for sl in list(st):
                if not step(sl):
                    s_ = st.pop(sl)
                    if s_["pre"] is not None:
                        begin(sl, s_["pre"][0], 1 - s_["par"], True)
                    else:
                        x = next(it, None)
                        if x is not None:
                            begin(sl, x, 1 - s_["par"], False)
                    if sl in st:
                        step(sl)

    def run_pair(self, fa, fb):
        lists = []
        for sl, f in enumerate((fa, fb)):
            self.cur, self.slot = [], sl
            if f is not None:
                f()
            lists.append(self.cur)
            self.cur, self.slot = None, None
        ia = ib = 0
        la, lb = lists
        while ia < len(la) or ib < len(lb):
            if ia < len(la):
                kind, args = la[ia]
                ia += 1
                (self.S.op if kind == "op" else self.S.dma)(*args)
            if ib < len(lb):
                kind, args = lb[ib]
                ib += 1
                (self.S.op if kind == "op" else self.S.dma)(*args)

    def phase_begin(self):
        self.S.barrier()
        self.stack = ExitStack()

    def sub_begin(self):
        self.S.barrier()
        self._outer = self.stack
        self.stack = ExitStack()

    def sub_end(self):
        self.S.barrier()
        self.stack.close()
        self.stack = self._outer

    def phase_end(self):
        self.S.barrier()
        self.stack.close()
        self.stack = None

    def op(self, e, fn, reads=(), writes=()):
        args = (e, fn, [r.k if isinstance(r, T) else r for r in reads if r is not None],
                [w.k if isinstance(w, T) else w for w in writes if w is not None])
        if self.cur is not None:
            self.cur.append(("op", args))
        else:
            self.S.op(*args)

    def dma(self, q, out_ap, in_ap, reads=(), writes=()):
        args = (q, out_ap, in_ap, [r.k if isinstance(r, T) else r for r in reads if r is not None],
                [w.k if isinstance(w, T) else w for w in writes if w is not None])
        if self.cur is not None:
            self.cur.append(("dma", args))
        else:
            self.S.dma(*args)

    def load(self, out_ap, in_ap, reads=(), writes=()):
        self.dma("sp", out_ap, in_ap, reads, writes)

    def store(self, out_ap, in_ap, reads=(), writes=()):
        self.dma(self.store_q, out_ap, in_ap, reads, writes)

    def mm(self, bankT, out_ap, lhsT, rhs, start, stop, reads):
        self.op("pe", lambda e: e.matmul(out_ap, lhsT=lhsT, rhs=rhs, start=start, stop=stop,
                                         skip_group_check=True),
                reads=reads, writes=[bankT])

    def transp(self, bankT, out_ap, in_ap, ident_ap, reads):
        self.op("pe", lambda e: e.transpose(out=out_ap, in_=in_ap, identity=ident_ap),
                reads=reads, writes=[bankT])

    def act(self, out_ap, in_ap, func, reads, writes, scale=None, bias=None, accum=None):
        kw = {}
        if scale is not None:
            kw["scale"] = scale
        if bias is not None:
            kw["bias"] = bias
        if accum is not None:
            kw["accum_out"] = accum
        self.op("act", lambda e: e.activation(out=out_ap, in_=in_ap, func=func, **kw), reads, writes)

    def tt(self, eng, out_ap, in0, in1, op, reads, writes):
        self.op(eng, lambda e: e.tensor_tensor(out=out_ap, in0=in0, in1=in1, op=op), reads, writes)

    def ts(self, eng, out_ap, in0, s1, s2, op0, op1, reads, writes):
        if op1 is None:
            self.op(eng, lambda e: e.tensor_scalar(out=out_ap, in0=in0, scalar1=s1, scalar2=None, op0=op0),
                    reads, writes)
        else:
            self.op(eng, lambda e: e.tensor_scalar(out=out_ap, in0=in0, scalar1=s1, scalar2=s2,
                                                   op0=op0, op1=op1), reads, writes)

    def copy(self, eng, out_ap, in_ap, reads, writes):
        if eng == "act":
            self.op("act", lambda e: e.copy(out=out_ap, in_=in_ap), reads, writes)
        else:
            self.op(eng, lambda e: e.tensor_copy(out=out_ap, in_=in_ap), reads, writes)

    def setup_consts(self):
        nc = self.nc
        self.cstack = ExitStack()
        old = self.stack
        self.stack = self.cstack
        self.ident = self.sb("ident", [128, 128], BF16)
        self.identf = self.sb("identf", [128, 128], F32)
        self.UI = self.sb("UI", [128, 128], F32)
        self.LI = self.sb("LI", [128, 128], F32)
        self.SL = self.sb("SL", [128, 128], F32)
        self.SU = self.sb("SU", [128, 128], F32)
        self.ones = self.sb("ones", [128, 128], F32)
        self.onesb = self.sb("onesb", [128, 128], BF16)
        self.eps = self.sb("eps", [128, 1], F32)
        self.one1 = self.sb("one1", [128, 1], F32)
        self.mhalf = self.sb("mhalf", [128, 1], F32)

        def tri(Tt, pat, cm, cmp):
            self.op("pool", lambda e: e.memset(Tt[:], 1.0), [], [Tt])
            self.op("pool", lambda e: e.affine_select(out=Tt[:], in_=Tt[:], pattern=[[pat, 128]],
                                                      compare_op=cmp, fill=0.0, base=0,
                                                      channel_multiplier=cm), [Tt], [Tt])
        tri(self.identf, 1, -1, ALU.is_equal)
        tri(self.UI, 1, -1, ALU.is_ge)
        tri(self.LI, -1, 1, ALU.is_ge)
        tri(self.SL, -1, 1, ALU.is_gt)
        tri(self.SU, 1, -1, ALU.is_gt)
        self.op("pool", lambda e: e.memset(self.ones[:], 1.0), [], [self.ones])
        self.op("pool", lambda e: e.memset(self.onesb[:], 1.0), [], [self.onesb])
        self.op("pool", lambda e: e.memset(self.eps[:], EPS), [], [self.eps])
        self.op("pool", lambda e: e.memset(self.one1[:], 1.0), [], [self.one1])
        self.op("pool", lambda e: e.memset(self.mhalf[:], -0.5), [], [self.mhalf])
        self.copy("dve", self.ident[:], self.identf[:], [self.identf], [self.ident])
        self.stack = old

    def bcast_load(self, dst, src_row_ap, n):
        self.load(dst[:, 0:n], src_row_ap.partition_broadcast(128), [], [dst])

    def rstd_from(self, ssum, n, out_rstd, tmp):
        self.act(tmp, ssum, AF.Sqrt, [], [], scale=1.0 / n, bias=self.eps[:, 0:1])

    def layer_norm_tile(self, x, xT, w, out_ap, outT, tmp_pool):
        st = tmp_pool
        junk, s1, s2, mv = st["junk"], st["s1"], st["s2"], st["mv"]
        self.act(junk[:, 0:w], x, AF.Identity, [xT], [junk, s1], accum=s1[:, 0:1])
        self.act(junk[:, 0:w], x, AF.Square, [xT], [junk, s2], accum=s2[:, 0:1])
        self.ts("dve", mv[:, 0:1], s1[:, 0:1], 1.0 / w, None, ALU.mult, None, [s1], [mv])
        self.ts("dve", mv[:, 1:2], s2[:, 0:1], 1.0 / w, None, ALU.mult, None, [s2], [mv])
        self.tt("dve", mv[:, 2:3], mv[:, 0:1], mv[:, 0:1], ALU.mult, [mv], [mv])
        self.op("dve", lambda e: e.scalar_tensor_tensor(out=mv[:, 4:5], in0=mv[:, 1:2], scalar=EPS,
                                                        in1=mv[:, 2:3], op0=ALU.add, op1=ALU.subtract),
                [mv], [mv])
        self.tt("pool", mv[:, 5:6], mv[:, 4:5], self.mhalf[:, 0:1], ALU.pow, [mv, self.mhalf], [mv])
        self.op("dve", lambda e: e.scalar_tensor_tensor(out=mv[:, 6:7], in0=mv[:, 0:1], scalar=-1.0,
                                                        in1=mv[:, 5:6], op0=ALU.mult, op1=ALU.mult),
                [mv], [mv])
        self.act(out_ap, x, AF.Identity, [xT, mv], [outT], scale=mv[:, 5:6], bias=mv[:, 6:7])

    def ln_tmps(self):
        return {"junk": self.sb("lnjunk", [128, D], F32), "s1": self.sb("lns1", [128, 1], F32),
                "s2": self.sb("lns2", [128, 1], F32), "mv": self.sb("lnmv", [128, 8], F32)}


class Model(Builder):
    def declare(self):
        L, NT = self.DEPTH, self.NT
        di = self.dram_in
        self.x_in = di("x_in", [NT, D])
        self.c_lay = di("c_lay", [128, 8, 2])
        self.w_mod = di("w_mod", [L, D, 6 * D])
        self.b_mod = di("b_mod", [L, 6 * D])
        self.w_in = di("w_in", [L, D, IN_W])
        self.b_gate = di("b_gate", [L, GATE_IN])
        self.w_uq = di("w_uq", [L, MLA_QR, MLA_H * MLA_QK])
        self.g_q_mla = di("g_q_mla", [L, MLA_QR])
        self.w_ukv = di("w_ukv", [L, MLA_KVR, MLA_H * 128])
        self.g_kv_mla = di("g_kv_mla", [L, MLA_KVR])
        self.conv_w_l = di("conv_w_l", [L, 128, 12, 3])
        self.conv_b_l = di("conv_b_l", [L, 128, 12])
        self.a_log = di("a_log", [L, 32])
        self.dt_bias = di("dt_bias", [L, 32])
        self.d_skip = di("d_skip", [L, 16])
        self.g_ssd = di("g_ssd", [L, SSD_INNER])
        self.g_q_gqa = di("g_q_gqa", [L, 64])
        self.g_k_gqa = di("g_k_gqa", [L, 64])
        self.w_out = di("w_out", [L, OUT_W, D])
        self.ln1_g = di("ln1_g", [L, D])
        self.ln1_b = di("ln1_b", [L, D])
        self.w_ffn_in = di("w_ffn_in", [L, D, 2 * FFN_H])
        self.w_ffn_out = di("w_ffn_out", [L, FFN_H, D])
        self.ln2_g = di("ln2_g", [L, D])
        self.ln2_b = di("ln2_b", [L, D])
        self.rope_m = di("rope_m", [NT, 2, 16])
        self.rope_g = di("rope_g", [NT, 2, 32])
        self.out = self.nc.dram_tensor("out", [self.SEQ, D], F32, kind="ExternalOutput").ap()
        ds = self.dram_scr
        self.MOD, self.kMOD = ds("MOD", [L, 2, 6 * D], F32)
        self.P, self.kP = ds("P", [NT, IN_W], BF16)
        self.DT, self.kDT = ds("DT", [NT, 32], F32)
        self.CV, self.kCV = ds("CV", [12, 128, NT], BF16)
        self.QTm, self.kQTm = ds("QTm", [MLA_H, MLA_QK, NT], BF16)
        self.KTm, self.kKTm = ds("KTm", [MLA_H, MLA_QK, NT], BF16)
        self.Vm, self.kVm = ds("Vm", [NT, MLA_H, 64], BF16)
        self.QTg, self.kQTg = ds("QTg", [GQA_H, 64, NT], BF16)
        self.KTg, self.kKTg = ds("KTg", [GQA_KV, 64, NT], BF16)
        self.Vg, self.kVg = ds("Vg", [NT, GQA_KV, 64], BF16)
        self.YT, self.kYT = ds("YT", [16, 128, NT], BF16)
        self.YF, self.kYF = ds("YF", [NT, SSD_INNER], F32)
        self.X1, self.kX1 = ds("X1", [NT, D], F32)
        self.X2, self.kX2 = ds("X2", [NT, D], F32)
        self.HID, self.kHID = ds("HID", [FFN_H // 128, 128, NT], BF16)
        self.YS, self.kYS = ds("YS", [NT, SSD_INNER], BF16)

    def mod_setup(self, li):
        cl = self.sb("cl", [128, 8, 2], F32)
        sc = self.sb("sc", [128, 8, 2], F32)
        bm = self.sb("bm", [2, 6 * D], F32)
        res = self.sb("modres", [2, 6 * D], F32)
        wst = [self.sb("wmod_st", [128, 8, 512], F32) for _ in range(2)]

        def main():
            self.load(cl[:], self.c_lay, [], [cl])
            self.act(sc[:], cl[:], AF.Silu, [cl], [sc])
            self.load(bm[:], self.b_mod[li:li + 1, :].to_broadcast([2, 6 * D]), [], [bm])
            for j in range(12):
                w = wst[j % 2]
                self.load(w[:], self.w_mod[li, :, j * 512:(j + 1) * 512].rearrange("(kc p) n -> p kc n", p=128),
                          [], [w])
                b = self.bank()
                for kc in range(8):
                    self.mm(b, b[0:2, :], sc[:, kc, :], w[:, kc, :], kc == 0, kc == 7, [sc, w])
                self.tt("dve", res[:, j * 512:(j + 1) * 512], b[0:2, :], bm[:, j * 512:(j + 1) * 512], ALU.add,
                        [b, bm], [res, b])
            for c0 in (D, 4 * D):
                self.ts("dve", res[:, c0:c0 + D], res[:, c0:c0 + D], 1.0, None, ALU.add, None, [res], [res])
            self.store(self.MOD[li], res[:], [res], [])
        return main

    def ph_mod(self, li):
        self.phase_begin()
        self.mod_setup(li)()
        self.phase_end()

    def mod_rows(self, dst, which, col0):
        self.load(dst[:], self.MOD[self.mod_li, which:which + 1, col0:col0 + D].to_broadcast([128, D]), [], [dst])

    def ln_mod_T(self, xt, tmps, scb, shb, hn, dstT, dst_ap_fn):
        xn = tmps["xn"]
        self.layer_norm_tile(xt[:], xt, D, xn[:], xn, tmps)
        self.tt("dve", xn[:], xn[:], scb[:], ALU.mult, [xn, scb], [xn])
        self.tt("pool", hn[:], xn[:], shb[:], ALU.add, [xn, shb], [hn])
        b = self.bank()
        bv = b[:].bitcast(BF16)
        for kc in range(8):
            self.transp(b, bv[:, kc * 128:(kc + 1) * 128], hn[:, kc * 128:(kc + 1) * 128], self.ident[:],
                        [hn, self.ident])
        self.copy("act", dst_ap_fn(), bv[:, 0:1024].rearrange("p (a b) -> p a b", a=8), [b], [dstT, b])

    def ph_proj(self, li, src, ksrc):
        NT, NTILE, CTX = self.NT, self.NTILE, self.CTX
        self.phase_begin()
        hT = self.sb("hT", [128, 8, NT], BF16)
        self.sub_begin()
        TM = []
        for _ in range(3):
            tm = self.ln_tmps()
            tm["xn"] = self.sb("xn", [128, D], F32)
            TM.append(tm)
        scb = [self.sb("scb", [128, D], F32) for _ in range(2)]
        shb = [self.sb("shb", [128, D], F32) for _ in range(2)]
        for w in range(2):
            self.mod_rows(scb[w], w, 1 * D)
            self.mod_rows(shb[w], w, 0)
        xts = [self.sb("xt", [128, D], F32) for _ in range(3)]
        hns = [self.sb("hn", [128, D], BF16) for _ in range(3)]

        blocks = []
        for (a, bnd) in ((0, O_XBC), (O_GQA, O_GATE), (O_GATE, IN_W)):
            c = a
            while c < bnd:
                blocks.append((c, min(c + 512, bnd), False))
                c += 512
        blocks.append((O_DT, O_DT + 32, True))
        wbf = [self.sb("win_bf", [128, 8, 512], BF16) for _ in range(3)]
        ev = [self.sb("pev", [128, 512], BF16) for _ in range(4)]
        evf = [self.sb("pevf", [128, 32], F32) for _ in range(2)]
        bgp = self.sb("bgp", [128, GATE_IN], F32)
        self.load(bgp[:], self.b_gate[li:li + 1, :].to_broadcast([128, GATE_IN]), [], [bgp])

        def wload(bi_):
            c0, c1, _ = blocks[bi_]
            wb = wbf[bi_ % 3]
            self.dma("pool", wb[:, :, 0:c1 - c0], self.w_in[li, :, c0:c1].rearrange("(kc p) n -> p kc n", p=128),
                     [], [wb])
        wload(0)
        wload(1)

        def ln_body(t, sl):
            xt, hn = xts[sl], hns[sl]
            w = 1 if t < self.CT_TILES else 0
            self.load(xt[:], src[t * 128:(t + 1) * 128, :], [ksrc], [xt])
            self.ln_mod_T(xt, TM[sl], scb[w], shb[w], hn, hT, lambda t=t: hT[:, :, t * 128:(t + 1) * 128])
        self.run_streams(range(NTILE), ln_body, 3)
        for bi_, (c0, c1, isdt) in enumerate(blocks):
            wd = c1 - c0
            wb = wbf[bi_ % 3]
            if bi_ + 2 < len(blocks):
                wload(bi_ + 2)
            for t in range(NTILE):
                b = self.bank()
                for kc in range(8):
                    self.mm(b, b[:, 0:wd], hT[:, kc, t * 128:(t + 1) * 128], wb[:, kc, 0:wd], kc == 0, kc == 7,
                            [hT, wb])
                if isdt:
                    e = evf[t % 2]
                    self.copy("dve", e[:, 0:wd], b[:, 0:wd], [b], [e, b])
                    self.store(self.DT[t * 128:(t + 1) * 128, :], e[:, 0:wd], [e], [self.kDT])
                else:
                    e = ev[t % 4]
                    z0, z1 = max(c0, O_Z), min(c1, O_Z + SSD_INNER)
                    if z0 < z1:
                        if z0 > c0:
                            self.copy("dve", e[:, 0:z0 - c0], b[:, 0:z0 - c0], [b], [e, b])
                        self.act(e[:, z0 - c0:z1 - c0], b[:, z0 - c0:z1 - c0], AF.Silu, [b], [e, b])
                        if z1 < c1:
                            self.copy("dve", e[:, z1 - c0:wd], b[:, z1 - c0:wd], [b], [e, b])
                    elif c0 >= O_GATE:
                        self.tt("dve", e[:, 0:wd], b[:, 0:wd], bgp[:, c0 - O_GATE:c1 - O_GATE], ALU.add, [b, bgp],
                                [e, b])
                    else:
                        self.copy("act" if t % 2 else "dve", e[:, 0:wd], b[:, 0:wd], [b], [e, b])
                    self.store(self.P[t * 128:(t + 1) * 128, c0:c1], e[:, 0:wd], [e], [self.kP])
        self.sub_end()
        self.sub_begin()
        wbf = [self.sb("win_bf2", [128, 8, 128], BF16) for _ in range(3)]

        def cwload(j):
            c0 = O_XBC + j * 128
            self.dma("pool", wbf[j % 3][:], self.w_in[li, :, c0:c0 + 128].rearrange("(kc p) n -> p kc n", p=128),
                     [], [wbf[j % 3]])
        cwload(0)
        cwload(1)
        cw = self.sb("cw", [128, 12, 3], F32)
        cb = self.sb("cb", [128, 12], F32)
        self.load(cw[:], self.conv_w_l[li], [], [cw])
        self.load(cb[:], self.conv_b_l[li], [], [cb])
        U = [self.sb("U", [128, NT + 3], F32) for _ in range(2)]
        A = [self.sb("A", [128, NT + 1], F32) for _ in range(2)]
        CVs = [self.sb("CVs", [128, NT + 1], BF16) for _ in range(2)]
        for u in U:
            self.op("pool", lambda e, u=u: e.memset(u[:], 0.0), [], [u])
        tblocks = [(0, CTX)] + [(CTX + i * 512, min(CTX + (i + 1) * 512, NT)) for i in range((self.SEQ + 511) // 512)]
        W = NT + 1
        for j in range(12):
            c0 = O_XBC + j * 128
            wb = wbf[j % 3]
            u, a, cv = U[j % 2], A[j % 2], CVs[j % 2]
            if j + 2 < 12:
                cwload(j + 2)
            for (t0, t1) in tblocks:
                n = t1 - t0
                b = self.bank()
                for kc in range(8):
                    self.mm(b, b[:, 0:n], wb[:, kc, 0:128], hT[:, kc, t0:t1], kc == 0, kc == 7, [hT, wb])
                off = 1 + t0 if t0 < CTX else 2 + t0
                self.copy("act" if (t0 // 512) % 2 else "dve", u[:, off:off + n], b[:, 0:n], [b], [u, b])
            self.act(a[:], u[:, 1:1 + W], AF.Identity, [u, cw, cb], [a], scale=cw[:, j, 1:2], bias=cb[:, j:j + 1])
            self.op("dve", lambda e, a=a, u=u, j=j: e.scalar_tensor_tensor(
                out=a[:], in0=u[:, 0:W], scalar=cw[:, j, 0:1], in1=a[:], op0=ALU.mult, op1=ALU.add),
                [u, cw, a], [a])
            self.op("dve", lambda e, a=a, u=u, j=j: e.scalar_tensor_tensor(
                out=a[:], in0=u[:, 2:2 + W], scalar=cw[:, j, 2:3], in1=a[:], op0=ALU.mult, op1=ALU.add),
                [u, cw, a], [a])
            self.act(cv[:], a[:], AF.Silu, [a], [cv])
            self.store(self.CV[j, :, 0:CTX], cv[:, 0:CTX], [cv], [self.kCV])
            self.store(self.CV[j, :, CTX:NT], cv[:, CTX + 1:NT + 1], [cv], [self.kCV])
        self.store_q = "sp"
        self.sub_end()
        self.phase_end()

    def rms_scale(self, ssum_ap, n, w, tmp, out, reads):
        self.ts("dve", tmp[:, 0:w], ssum_ap, 1.0 / n, EPS, ALU.mult, ALU.add, list(reads), [tmp])
        self.tt("pool", out[:, 0:w], tmp[:, 0:w], self.mhalf[:, 0:1].to_broadcast([128, w]), ALU.pow,
                [tmp, self.mhalf], [out])

    def rope(self, d1, d2, dstT, x1, x2, srcT, cos, sin, tabT, tmps, H, hf):
        cb_ = cos.unsqueeze(1).to_broadcast([128, H, hf])
        sb_ = sin.unsqueeze(1).to_broadcast([128, H, hf])
        ta, tb = tmps
        va = ta[:, 0:H * hf].rearrange("p (h f) -> p h f", h=H)
        vb = tb[:, 0:H * hf].rearrange("p (h f) -> p h f", h=H)
        self.tt("dve", va, x1, cb_, ALU.mult, [srcT, tabT], [ta])
        self.tt("pool", vb, x2, sb_, ALU.mult, [srcT, tabT], [tb])
        self.tt("dve", d1, va, vb, ALU.subtract, [ta, tb], [dstT])
        self.tt("dve", va, x1, sb_, ALU.mult, [srcT, tabT], [ta])
        self.tt("pool", vb, x2, cb_, ALU.mult, [srcT, tabT], [tb])
        self.tt("dve", d2, va, vb, ALU.add, [ta, tb], [dstT])

    def heads_T(self, src, srcT, H, d, dstD, t):
        b = self.bank()
        bv = b[:].bitcast(BF16)
        for h in range(H):
            self.transp(b, bv[0:d, h * 128:(h + 1) * 128], src[:, h, :], self.ident[:], [srcT, self.ident])
        o = self.hT_out[self.hT_i % 2]
        self.hT_i += 1
        self.copy("act", o[0:d, 0:H, :], bv[0:d, 0:H * 128].rearrange("p (h n) -> p h n", h=H), [b], [o, b])
        self.store(dstD[:, :, t * 128:(t + 1) * 128].rearrange("h d n -> d h n"), o[0:d, 0:H, :], [o], [])

    def ph_post(self, li):
        NT, NTILE = self.NT, self.NTILE
        self.phase_begin()
        wuq = self.sb("wuq", [128, 3, 768], BF16)
        wkv = self.sb("wkv", [128, 2, 1024], BF16)
        self.dma("pool", wuq[:], self.w_uq[li].rearrange("(kc p) n -> p kc n", p=128), [], [wuq])
        self.dma("pool", wkv[:], self.w_ukv[li].rearrange("(kc p) n -> p kc n", p=128), [], [wkv])
        gq = self.sb("gq", [128, MLA_QR], F32)
        gkv = self.sb("gkv", [128, MLA_KVR], F32)
        gqg = self.sb("gqg", [128, 64], F32)
        gkg = self.sb("gkg", [128, 64], F32)
        self.load(gq[:], self.g_q_mla[li:li + 1, :].to_broadcast([128, MLA_QR]), [], [gq])
        self.load(gkv[:], self.g_kv_mla[li:li + 1, :].to_broadcast([128, MLA_KVR]), [], [gkv])
        self.load(gqg[:], self.g_q_gqa[li:li + 1, :].to_broadcast([128, 64]), [], [gqg])
        self.load(gkg[:], self.g_k_gqa[li:li + 1, :].to_broadcast([128, 64]), [], [gkg])
        def mk():
            d_ = {}
            d_["hTo"] = [self.sb("hTo", [128, 8, 128], BF16) for _ in range(2)]
            d_["in"] = [(self.sb("pm", [128, MLA_IN], BF16), self.sb("pg", [128, GQA_IN], BF16),
                         self.sb("rm", [128, 2, 16], F32), self.sb("rg", [128, 2, 32], F32)) for _ in range(2)]
            d_["junk"] = self.sb("junk", [128, 512], F32)
            d_["ss"] = self.sb("ss", [128, 16], F32)
            d_["sq"] = self.sb("sqt", [128, 16], F32)
            d_["rs"] = self.sb("rs", [128, 16], F32)
            d_["qn"] = self.sb("qn", [128, MLA_QR], BF16)
            d_["qnT"] = self.sb("qnT", [128, 3, 128], BF16)
            d_["qs"] = self.sb("qs", [128, 8, 96], F32)
            d_["qf"] = self.sb("qf", [128, 8, 96], BF16)
            d_["kf"] = self.sb("kf", [128, 8, 96], BF16)
            d_["vt"] = self.sb("vt", [128, 8, 64], BF16)
            d_["kr"] = self.sb("kr", [128, 32], F32)
            d_["krb"] = self.sb("krb", [128, 32], BF16)
            d_["ta"] = self.sb("ta", [128, 256], F32)
            d_["tb"] = self.sb("tb", [128, 256], F32)
            d_["gq_f"] = self.sb("gq_f", [128, 8, 64], F32)
            d_["gq_b"] = self.sb("gq_b", [128, 8, 64], BF16)
            d_["gk_f"] = self.sb("gk_f", [128, 2, 64], F32)
            d_["gk_b"] = self.sb("gk_b", [128, 2, 64], BF16)
            return d_
        TS = [mk(), mk(), mk()]
        self.hT_i = 0
        def loads(t, sl, par):
            pm, pg, rm, rg = TS[sl]["in"][par]
            r0, r1 = t * 128, (t + 1) * 128
            self.load(pm[:], self.P[r0:r1, 0:MLA_IN], [], [pm])
            self.load(pg[:], self.P[r0:r1, O_GQA:O_GATE], [], [pg])
            self.load(rm[:], self.rope_m[r0:r1], [], [rm])
            self.load(rg[:], self.rope_g[r0:r1], [], [rg])

        def body(t, sl, par):
            d_ = TS[sl]
            self.hT_out = d_["hTo"]
            pm, pg, rm, rg = d_["in"][par]
            junk, ss, sq, rs, qn, qnT, qs, qf, kf, vt = (d_[k] for k in ("junk", "ss", "sq", "rs", "qn", "qnT",
                                                                        "qs", "qf", "kf", "vt"))
            kr, krb, ta, tb, gq_f, gq_b, gk_f, gk_b = (d_[k] for k in ("kr", "krb", "ta", "tb", "gq_f", "gq_b",
                                                                       "gk_f", "gk_b"))
            r0, r1 = t * 128, (t + 1) * 128
            self.act(junk[:, 0:MLA_QR], pm[:, 0:MLA_QR], AF.Square, [pm], [junk, ss], accum=ss[:, 0:1])
            self.act(junk[:, 0:MLA_KVR], pm[:, MLA_QR:MLA_QR + MLA_KVR], AF.Square, [pm], [junk, ss],
                     accum=ss[:, 1:2])
            self.rms_scale(ss[:, 0:1], MLA_QR, 1, sq, rs, [ss])
            self.op("dve", lambda e, pm=pm: e.scalar_tensor_tensor(
                out=qn[:], in0=pm[:, 0:MLA_QR], scalar=rs[:, 0:1], in1=gq[:], op0=ALU.mult, op1=ALU.mult),
                [pm, rs, gq], [qn])
            b = self.bank()
            bv = b[:].bitcast(BF16)
            for c in range(3):
                self.transp(b, bv[:, c * 128:(c + 1) * 128], qn[:, c * 128:(c + 1) * 128], self.ident[:],
                            [qn, self.ident])
            self.copy("act", qnT[:], bv[:, 0:384].rearrange("p (a b) -> p a b", a=3), [b], [qnT, b])
            for (h0, h1) in ((0, 5), (5, 8)):
                b = self.bank()
                wd = (h1 - h0) * 96
                for c in range(3):
                    self.mm(b, b[:, 0:wd], qnT[:, c, :], wuq[:, c, h0 * 96:h1 * 96], c == 0, c == 2, [qnT, wuq])
                self.copy("act", qs[:, h0:h1, :], b[:, 0:wd].rearrange("p (h f) -> p h f", f=96), [b], [qs, b])
            self.copy("pool", qf[:, :, 0:64], qs[:, :, 0:64], [qs], [qf])
            self.rope(qf[:, :, 64:80], qf[:, :, 80:96], qf, qs[:, :, 64:80], qs[:, :, 80:96], qs,
                      rm[:, 0, :], rm[:, 1, :], rm, (ta, tb), 8, 16)
            self.heads_T(qf, qf, 8, 96, self.QTm, t)
            self.rms_scale(ss[:, 1:2], MLA_KVR, 1, sq, rs, [ss])
            self.op("dve", lambda e, pm=pm: e.scalar_tensor_tensor(
                out=qn[:, 0:MLA_KVR], in0=pm[:, MLA_QR:MLA_QR + MLA_KVR], scalar=rs[:, 0:1], in1=gkv[:],
                op0=ALU.mult, op1=ALU.mult), [pm, rs, gkv], [qn])
            b = self.bank()
            bv = b[:].bitcast(BF16)
            for c in range(2):
                self.transp(b, bv[:, c * 128:(c + 1) * 128], qn[:, c * 128:(c + 1) * 128], self.ident[:],
                            [qn, self.ident])
            self.copy("act", qnT[:, 0:2, :], bv[:, 0:256].rearrange("p (a b) -> p a b", a=2), [b], [qnT, b])
            for hb in range(2):
                b = self.bank()
                for c in range(2):
                    self.mm(b, b[:, 0:512], qnT[:, c, :], wkv[:, c, hb * 512:(hb + 1) * 512], c == 0, c == 1,
                            [qnT, wkv])
                v4 = b[:, 0:512].rearrange("p (h f) -> p h f", f=128)
                self.copy("act", kf[:, hb * 4:(hb + 1) * 4, 0:64], v4[:, :, 0:64], [b], [kf, b])
                self.copy("dve", vt[:, hb * 4:(hb + 1) * 4, :], v4[:, :, 64:128], [b], [vt, b])
            self.copy("dve", kr[:], pm[:, MLA_QR + MLA_KVR:MLA_IN], [pm], [kr])
            self.rope(krb[:, 0:16].unsqueeze(1), krb[:, 16:32].unsqueeze(1), krb, kr[:, 0:16].unsqueeze(1),
                      kr[:, 16:32].unsqueeze(1), kr, rm[:, 0, :], rm[:, 1, :], rm, (ta, tb), 1, 16)
            self.copy("pool", kf[:, :, 64:96], krb[:].unsqueeze(1).to_broadcast([128, 8, 32]), [krb], [kf])
            self.heads_T(kf, kf, 8, 96, self.KTm, t)
            self.store(self.Vm[r0:r1], vt[:], [vt], [])
            for (nm, H, c0, gvec, xf, xb, dst) in (("q", 8, 0, gqg, gq_f, gq_b, self.QTg),
                                                    ("k", 2, 512, gkg, gk_f, gk_b, self.KTg)):
                src = pg[:, c0:c0 + H * 64].rearrange("p (h f) -> p h f", h=H)
                jv = junk[:, 0:H * 64].rearrange("p (h f) -> p h f", h=H)
                self.tt("dve", jv, src, src, ALU.mult, [pg], [junk])
                self.op("dve", lambda e, jv=jv, H=H: e.tensor_reduce(out=ss[:, 2:2 + H], in_=jv, axis=AX.X,
                                                                      op=ALU.add), [junk], [ss])
                self.rms_scale(ss[:, 2:2 + H], 64, H, sq, rs, [ss])
                self.tt("dve", xf[:], src, rs[:, 0:H].unsqueeze(2).to_broadcast([128, H, 64]), ALU.mult,
                        [pg, rs], [xf])
                self.tt("pool", xf[:], xf[:], gvec[:].unsqueeze(1).to_broadcast([128, H, 64]), ALU.mult,
                        [xf, gvec], [xf])
                self.rope(xb[:, :, 0:32], xb[:, :, 32:64], xb, xf[:, :, 0:32], xf[:, :, 32:64], xf,
                          rg[:, 0, :], rg[:, 1, :], rg, (ta, tb), H, 32)
                self.heads_T(xb, xb, H, 64, dst, t)
            self.store(self.Vg[r0:r1].rearrange("n h f -> n (h f)"), pg[:, 640:768], [pg], [])
        if li + 1 < self.DEPTH:
            fm = self.mod_setup(li + 1)
            lp = self.record(lambda: self.run_streams(range(NTILE), body, 3, loads=loads), None)
            lm = self.record(fm, [6, 7])
            self.merge_emit([lp, lm])
        else:
            self.run_streams(range(NTILE), body, 3, loads=loads)
        self.phase_end()

    def attn_setup(self, li, QT, KT, V, H, kv_of, d, scale, ychunk0, with_ctx):
        NT, NTILE, CTX = self.NT, self.NTILE, self.CTX
        KTs = [self.sb("KTs", [128, NT], BF16) for _ in range(2)]
        V1s = [self.sb("V1s", [128, NTILE, 128], BF16) for _ in range(2)]
        QTs = [self.sb("QTs", [128, 512], BF16) for _ in range(3)]
        PTs = [self.sb("PTs", [128, 512], BF16) for _ in range(6)]
        dlos = [self.sb("dlo", [64, 512], F32) for _ in range(2)]
        osbs = [self.sb("osb", [128, 512], F32) for _ in range(2)]
        yv = [self.sb("yv", [64, 512], BF16) for _ in range(2)]
        for v1 in V1s:
            self.op("pool", lambda e, v1=v1: e.memset(v1[:], 0.0), [], [v1])
            self.op("pool", lambda e, v1=v1: e.memset(v1[:, :, 64:128], 1.0), [v1], [v1])
        for t_ in KTs + QTs:
            self.op("pool", lambda e, t_=t_: e.memset(t_[:], 0.0), [], [t_])
        qblocks = [(CTX + i * 512, min(CTX + (i + 1) * 512, NT), NTILE) for i in range((self.SEQ + 511) // 512)]
        if with_ctx:
            qblocks = [(0, CTX, self.CT_TILES)] + qblocks
        blocks = [(h, q0, q1, nkt) for h in range(H) for (q0, q1, nkt) in qblocks]

        def load_head(h):
            kt_, v1 = KTs[h % 2], V1s[h % 2]
            self.load(kt_[0:d, :], KT[kv_of(h)], [], [kt_])
            vsrc = V[:, kv_of(h), :].rearrange("(kt p) f -> p kt f", p=128)
            for k0 in range(0, NTILE, 8):
                k1 = min(k0 + 8, NTILE)
                self.load(v1[:, k0:k1, 0:64], vsrc[:, k0:k1, :], [], [v1])

        def load_q(bi):
            h, q0, q1, nkt = blocks[bi]
            qt = QTs[bi % 3]
            self.load(qt[0:d, 0:q1 - q0], QT[h, :, q0:q1], [], [qt])

        def main():
          SK = 2
          pending = None
          load_head(0)
          load_q(0)
          for bi, (h, q0, q1, nkt) in enumerate(blocks):
              nq = q1 - q0
              kt_, v1 = KTs[h % 2], V1s[h % 2]
              qt = QTs[bi % 3]
              ob = self.banks[6]
              if bi + 1 < len(blocks):
                  if blocks[bi + 1][0] != h:
                      load_head(blocks[bi + 1][0])
                  load_q(bi + 1)
              sb_of = {}
              groups = [(k, min(k + 2, nkt)) for k in range(0, nkt, 2)]
              ng = len(groups)
              ocnt = [0]

              def S_(kt):
                  sbk = self.banks[kt % 4]
                  sb_of[kt] = sbk
                  self.mm(sbk, sbk[:, 0:nq], kt_[:, kt * 128:(kt + 1) * 128], qt[:, 0:nq], True, True,
                          [kt_, qt])

              def issue_S(g):
                  for kt in reversed(range(*groups[g])):
                      S_(kt)

              def issue_exp(g):
                  for kt in range(*groups[g]):
                      sbk = sb_of.pop(kt)
                      pt = PTs[kt % 6]
                      self.act(pt[:, 0:nq], sbk[:, 0:nq], AF.Exp, [sbk], [pt, sbk], scale=scale)

              def issue_O(g):
                  for kt in reversed(range(*groups[g])):
                      pt = PTs[kt % 6]
                      self.mm(ob, ob[:, 0:nq], v1[:, kt, :], pt[:, 0:nq], ocnt[0] == 0, ocnt[0] == nkt - 1,
                              [v1, pt])
                      ocnt[0] += 1

              issue_S(0)
              if pending is not None:
                  pending()
                  pending = None
              for g in range(ng):
                  if g + 1 < ng:
                      issue_S(g + 1)
                  issue_exp(g)
                  if g >= 1:
                      issue_O(g - 1)
              issue_O(ng - 1)
              dlo, osb, y = dlos[bi % 2], osbs[bi % 2], yv[bi % 2]
              self.copy("dve", osb[:, 0:nq], ob[:, 0:nq], [ob], [osb, ob])

              def epi(nq=nq, dlo=dlo, osb=osb, y=y, h=h, q0=q0, q1=q1):
                  self.load(dlo[0:64, 0:nq], osb[64:128, 0:nq], [osb], [dlo])
                  self.op("dve", lambda e: e.reciprocal(out=dlo[0:64, 0:nq], in_=dlo[0:64, 0:nq]), [dlo], [dlo])
                  self.tt("dve", y[:, 0:nq], osb[0:64, 0:nq], dlo[0:64, 0:nq], ALU.mult, [osb, dlo], [y])
                  self.store(self.YT[ychunk0 + h // 2, (h % 2) * 64:(h % 2) * 64 + 64, q0:q1], y[:, 0:nq], [y], [])
              pending = epi
          if pending is not None:
              pending()
        return main


    def ssd_setup(self, li, d):
        NT, NTILE, CT = self.NT, self.NTILE, self.CT_TILES
        if True:
            aneg = self.sb("aneg", [128, 32], F32)
            dtb = self.sb("dtb", [128, 32], F32)
            self.load(aneg[:], self.a_log[li:li + 1, :].to_broadcast([128, 32]), [], [aneg])
            self.act(aneg[:], aneg[:], AF.Exp, [aneg], [aneg])
            self.ts("dve", aneg[:], aneg[:], -1.0, None, ALU.mult, None, [aneg], [aneg])
            self.load(dtb[:], self.dt_bias[li:li + 1, :].to_broadcast([128, 32]), [], [dtb])
            if d == 1:
                dsk = self.sb("dsk", [128, 16], F32)
                gss = self.sb("gss", [128, SSD_INNER], F32)
                self.load(dsk[:], self.d_skip[li:li + 1, :].to_broadcast([128, 16]), [], [dsk])
                self.load(gss[:], self.g_ssd[li:li + 1, :].to_broadcast([128, SSD_INNER]), [], [gss])
                sz = self.sb("sz", [128, SSD_INNER], F32)
                fss = self.sb("fss", [128, 2], F32)
                frs = self.sb("frs", [128, 2], F32)

            Lseg, Rseg, Lac, Lde = ((self.SL, self.UI, self.UI, self.SL) if d == 0 else
                                    (self.SU, self.LI, self.LI, self.SU))
            dt_all = self.sb("dt_all", [128, NTILE, 16], F32)
            dtA_all = self.sb("dtA_all", [128, NTILE, 16], F32)
            E_all = self.sb("E_all", [128, NTILE, 48], F32)
            sdt_all = self.sb("sdt_all", [128, NTILE, 16], F32)
            dtr_all = self.sb("dtr_all", [128, NTILE, 32], F32)

            def prep():
                self.load(dtr_all[:], self.DT.rearrange("(t p) c -> p t c", p=128), [], [dtr_all])
                self.tt("dve", dt_all[:], dtr_all[:, :, d * 16:(d + 1) * 16],
                        dtb[:, d * 16:(d + 1) * 16].unsqueeze(1).to_broadcast([128, NTILE, 16]), ALU.add,
                        [dtr_all, dtb], [dt_all])
                self.act(dt_all[:], dt_all[:], AF.Exp, [dt_all], [dt_all])
                self.act(dt_all[:], dt_all[:], AF.Ln, [dt_all, self.one1], [dt_all], bias=self.one1[:, 0:1])
                self.tt("dve", dtA_all[:], dt_all[:],
                        aneg[:, d * 16:(d + 1) * 16].unsqueeze(1).to_broadcast([128, NTILE, 16]), ALU.mult,
                        [dt_all, aneg], [dtA_all])
                for t0 in range(0, NTILE, 32):
                    t1 = min(t0 + 32, NTILE)
                    nn = (t1 - t0) * 16
                    for k_, L_ in enumerate((Lac, Lde, self.ones)):
                        bk = self.bank()
                        self.mm(bk, bk[:, 0:nn], L_[:], dtA_all[:, t0:t1, :], True, True, [L_, dtA_all])
                        self.act(E_all[:, t0:t1, k_ * 16:(k_ + 1) * 16],
                                 bk[:, 0:nn].rearrange("p (t c) -> p t c", c=16), AF.Exp, [bk], [E_all, bk])
                self.tt("dve", sdt_all[:], dt_all[:], E_all[:, :, 16:32], ALU.mult, [dt_all, E_all], [sdt_all])

            def mk():
                d_ = {}
                d_["cx"] = self.sb("cvx", [128, 8, 128], BF16)
                d_["bt"] = self.sb("bct", [128, 4, 128], BF16)
                d_["dr"] = self.sb("dtr", [128, 32], F32)
                d_["Bm"] = self.sb("Bm", [128, 2, 128], BF16)
                d_["dt"] = self.sb("dt", [128, 16], F32)
                d_["dtA"] = self.sb("dtA", [128, 16], F32)
                d_["E"] = self.sb("E", [128, 48], F32)
                d_["sdt"] = self.sb("sdt", [128, 16], F32)
                d_["xdt"] = self.sb("xdt", [128, SSD_INNER], BF16)
                d_["xdd"] = self.sb("xdd", [128, SSD_INNER], BF16)
                d_["msc"] = [self.sb("msc", [128, 128], F32) for _ in range(2)]
                d_["lhs"] = [self.sb("lhs", [128, 8, 128], F32) for _ in range(2)]
                d_["Es"] = [[self.sb("Es", [128, 4, 128], F32) for _ in range(2)] for _ in range(2)]
                d_["M"] = [self.sb("M", [128, 8, 128], BF16) for _ in range(2)]
                d_["yds"] = [self.sb("yds", [128, 512], F32) for _ in range(2)]
                d_["ytmp"] = self.sb("ytmp", [128, 512], F32)
                return d_
            PB = [mk(), mk()]
            XS = [self.sb("xs", [128, SSD_INNER], BF16) for _ in range(3)]
            YA = [self.sb("yacc", [128, SSD_INNER], F32) for _ in range(3)]
            if d == 1:
                YFb = [self.sb("yf", [128, SSD_INNER], F32) for _ in range(3)]
                ZT = [self.sb("zs", [128, SSD_INNER], BF16) for _ in range(3)]
                YNB = [self.sb("ynb", [128, SSD_INNER], BF16) for _ in range(2)]
            hT = [self.sb("hT", [128, 512], F32) for _ in range(2)]
            hTb = [self.sb("hTb", [128, 512], BF16) for _ in range(2)]
            for g in range(2):
                self.op("pool", lambda e, h_=hT[g]: e.memset(h_[:], 0.0), [], [hT[g]])
                self.op("pool", lambda e, h_=hTb[g]: e.memset(h_[:], 0.0), [], [hTb[g]])
            Lseg, Rseg, Lac, Lde = ((self.SL, self.UI, self.UI, self.SL) if d == 0 else
                                    (self.SU, self.LI, self.LI, self.SU))
            order = (list(range(NTILE)) if d == 0 else
                     list(range(CT - 1, -1, -1)) + list(range(NTILE - 1, CT - 1, -1)))

            def stage1(it, t, d=d):
                p_ = PB[it % 2]
                cx, bt, Bm, xdt, xdd = (p_[k] for k in ("cx", "bt", "Bm", "xdt", "xdd"))
                xs = XS[it % 3]
                r0, r1 = t * 128, (t + 1) * 128
                self.load(cx[:], self.CV[0:8, :, r0:r1].rearrange("j p n -> p j n"), [], [cx])
                self.load(bt[:], self.CV[8:12, :, r0:r1].rearrange("j p n -> p j n"), [], [bt])
                if d == 1:
                    self.load(YFb[it % 3][:], self.YF[r0:r1, :], [], [YFb[it % 3]])
                    self.load(ZT[it % 3][:], self.P[r0:r1, O_Z:O_Z + SSD_INNER], [], [ZT[it % 3]])
                b = self.bank()
                bv = b[:].bitcast(BF16)
                for j in range(8):
                    self.transp(b, bv[:, j * 128:(j + 1) * 128], cx[:, j, :], self.ident[:], [cx, self.ident])
                self.copy("dve", xs[:], bv[:, 0:1024], [b], [xs, b])
                b = self.bank()
                bv = b[:].bitcast(BF16)
                for g in range(2):
                    self.transp(b, bv[:, g * 128:(g + 1) * 128], bt[:, g, :], self.ident[:], [bt, self.ident])
                self.copy("dve", Bm[:], bv[:, 0:256].rearrange("p (g n) -> p g n", g=2), [b], [Bm, b])
                xs3 = xs[:].rearrange("p (h f) -> p h f", h=16)
                self.tt("dve", xdt[:].rearrange("p (h f) -> p h f", h=16), xs3,
                        dt_all[:, t, :].unsqueeze(2).to_broadcast([128, 16, 64]), ALU.mult, [xs, dt_all], [xdt])
                self.tt("pool", xdd[:].rearrange("p (h f) -> p h f", h=16), xs3,
                        sdt_all[:, t, :].unsqueeze(2).to_broadcast([128, 16, 64]), ALU.mult, [xs, sdt_all], [xdd])

            def stage1g(it, t, g):
                p_ = PB[it % 2]
                bt, xdt, msc = p_["bt"], p_["xdt"], p_["msc"][g]
                if True:
                    lh, Mg = p_["lhs"][g], p_["M"][g]
                    scb_ = self.bank()
                    self.mm(scb_, scb_[:, 0:128], bt[:, g, :], bt[:, 2 + g, :], True, True, [bt])
                    self.tt("dve", msc[:], scb_[:, 0:128], Rseg[:], ALU.mult, [scb_, Rseg], [msc, scb_])
                    self.tt("pool", lh[:], Lseg[:].unsqueeze(1).to_broadcast([128, 8, 128]),
                            dtA_all[:, t, g * 8:(g + 1) * 8].unsqueeze(2).to_broadcast([128, 8, 128]), ALU.mult,
                            [Lseg, dtA_all], [lh])
                    for half in range(2):
                        sg = self.bank()
                        es = p_["Es"][g][half]
                        for i in range(4):
                            self.mm(sg, sg[:, i * 128:(i + 1) * 128], lh[:, half * 4 + i, :], Rseg[:], True, True,
                                    [lh, Rseg])
                        self.act(es[:], sg[:, 0:512].rearrange("p (h q) -> p h q", h=4), AF.Exp, [sg], [es, sg])
                        self.tt("dve", Mg[:, half * 4:(half + 1) * 4, :], es[:],
                                msc[:].unsqueeze(1).to_broadcast([128, 4, 128]), ALU.mult, [es, msc], [Mg])
                    yd = self.bank()
                    for h in range(8):
                        c0 = (g * 8 + h) * 64
                        self.mm(yd, yd[:, h * 64:(h + 1) * 64], Mg[:, h, :], xdt[:, c0:c0 + 64], True, True,
                                [Mg, xdt])
                    self.copy("dve", p_["yds"][g][:], yd[:, 0:512], [yd], [p_["yds"][g], yd])

            def stage2(it, t, d=d):
                p_ = PB[it % 2]
                bt, Bm, E, xdd, yds, ytmp = (p_[k] for k in ("bt", "Bm", "E", "xdd", "yds", "ytmp"))
                ya = YA[it % 3]
                r0, r1 = t * 128, (t + 1) * 128
                for g in range(2):
                    yo = self.bank()
                    self.mm(yo, yo[:, 0:512], bt[:, 2 + g, :], hTb[g][:], True, True, [bt, hTb[g]])
                    self.tt("dve", ytmp[:].rearrange("p (h f) -> p h f", h=8),
                            yo[:, 0:512].rearrange("p (h f) -> p h f", h=8),
                            E_all[:, t, g * 8:(g + 1) * 8].unsqueeze(2).to_broadcast([128, 8, 64]), ALU.mult,
                            [yo, E_all], [ytmp, yo])
                    self.tt("pool", ya[:, g * 512:(g + 1) * 512], ytmp[:], yds[g][:], ALU.add,
                            [ytmp, yds[g]], [ya])
                    Sb = self.bank()
                    self.mm(Sb, Sb[:, 0:512], Bm[:, g, :], xdd[:, g * 512:(g + 1) * 512], True, True, [Bm, xdd])
                    h3 = hT[g][:].rearrange("p (h f) -> p h f", h=8)
                    self.tt("pool", h3, h3,
                            E_all[:, t, 32 + g * 8:32 + (g + 1) * 8].unsqueeze(2).to_broadcast([128, 8, 64]),
                            ALU.mult, [hT[g], E_all, hTb[g]], [hT[g]])
                    self.tt("dve", hT[g][:], hT[g][:], Sb[:, 0:512], ALU.add, [hT[g], Sb], [hT[g], Sb])
                    self.copy("pool", hTb[g][:], hT[g][:], [hT[g]], [hTb[g]])
                if d == 0:
                    self.store(self.YF[r0:r1, :], ya[:], [ya], [])

            def stage3(it, t):
                ya, xs, yf, zt, ynb = YA[it % 3], XS[it % 3], YFb[it % 3], ZT[it % 3], YNB[it % 2]
                r0, r1 = t * 128, (t + 1) * 128
                xs3 = xs[:].rearrange("p (h f) -> p h f", h=16)
                self.tt("dve", ya[:], ya[:], yf[:], ALU.add, [ya, yf], [ya])
                self.tt("pool", sz[:].rearrange("p (h f) -> p h f", h=16), xs3,
                        dsk[:].unsqueeze(2).to_broadcast([128, 16, 64]), ALU.mult, [xs, dsk], [sz])
                self.tt("dve", ya[:], ya[:], sz[:], ALU.add, [ya, sz], [ya])
                self.tt("dve", ya[:], ya[:], zt[:], ALU.mult, [ya, zt], [ya])
                self.act(sz[:], ya[:], AF.Square, [ya], [sz, fss], accum=fss[:, 0:1])
                self.act(frs[:, 1:2], fss[:, 0:1], AF.Ln, [fss, self.eps], [frs], scale=1.0 / SSD_INNER,
                         bias=self.eps[:, 0:1])
                self.act(frs[:, 0:1], frs[:, 1:2], AF.Exp, [frs], [frs], scale=-0.5)
                self.op("dve", lambda e, ya=ya, ynb=ynb: e.scalar_tensor_tensor(
                    out=ynb[:], in0=ya[:], scalar=frs[:, 0:1], in1=gss[:], op0=ALU.mult, op1=ALU.mult),
                    [ya, frs, gss], [ynb])
                self.store(self.YS[r0:r1, :], ynb[:], [ynb], [])

            def rec(f, bankset):
                save = (self.cur, self.bankset, self.bsi)
                self.cur, self.bankset, self.bsi = [], bankset, 0
                f()
                lst = self.cur
                self.cur, self.bankset, self.bsi = save
                return lst

            def main():
                n = len(order)
                def s1_lists(it):
                    lc = rec(lambda: stage1(it, order[it]), [4])
                    lg0 = rec(lambda: stage1g(it, order[it], 0), [4])
                    lg1 = rec(lambda: stage1g(it, order[it], 1), [7])
                    wpos = {}
                    for i_, (kind, args) in enumerate(lc):
                        for tr_ in args[-1]:
                            wpos[id(tr_)] = i_
                    delay = 0
                    for j_, (kind, args) in enumerate(lg1):
                        for tr_ in list(args[-2]) + list(args[-1]):
                            if id(tr_) in wpos:
                                delay = max(delay, wpos[id(tr_)] + 1 - j_)
                    return lc + lg0, lg1, delay
                self.cur.extend(rec(prep, [5]))
                la, lb, _ = s1_lists(0)
                self.cur.extend(la)
                self.cur.extend(lb)
                for it in range(n + (1 if d == 1 else 0)):
                    subs, delays = [], []
                    if it < n:
                        subs.append(rec(lambda: stage2(it, order[it]), [5]))
                        delays.append(0)
                    if it + 1 < n:
                        la, lb, nc_ = s1_lists(it + 1)
                        subs.append(la)
                        delays.append(0)
                        subs.append(lb)
                        delays.append(nc_)
                    if d == 1 and it >= 1:
                        subs.append(rec(lambda: stage3(it - 1, order[it - 1]), [5]))
                        delays.append(0)
                    idx = [0] * len(subs)
                    rnd = 0
                    while any(idx[k] < len(subs[k]) for k in range(len(subs))):
                        for k in range(len(subs)):
                            if idx[k] < len(subs[k]) and rnd >= delays[k]:
                                self.cur.append(subs[k][idx[k]])
                                idx[k] += 1
                        rnd += 1
            return main

    def ln_affine_store(self, pre, tm, gb, bb_, outt, dst_ap):
        xn = tm["xn"]
        self.layer_norm_tile(pre[:], pre, D, xn[:], xn, tm)
        self.tt("dve", xn[:], xn[:], gb[:], ALU.mult, [xn, gb], [xn])
        self.tt("pool", outt[:], xn[:], bb_[:], ALU.add, [xn, bb_], [outt])
        self.store(dst_ap, outt[:], [outt], [])

    def ph_merge(self, li, src, with_ctx):
        NT, NTILE, CT = self.NT, self.NTILE, self.CT_TILES
        self.phase_begin()
        wout = self.sb("wout", [128, 16, D], BF16)
        woutk = [Trk("woutk%d" % i) for i in range(8)]
        for i in range(8):
            self.dma("pool", wout[:, 2 * i:2 * i + 2, :],
                     self.w_out[li, i * 256:(i + 1) * 256, :].rearrange("(kc p) n -> p kc n", p=128),
                     [], [woutk[i]])
        g1 = [self.sb("g1b", [128, D], F32) for _ in range(2)]
        for w in range(2):
            self.mod_rows(g1[w], w, 2 * D)
        lg = self.sb("lg", [128, D], F32)
        lb = self.sb("lb", [128, D], F32)
        self.load(lg[:], self.ln1_g[li:li + 1, :].to_broadcast([128, D]), [], [lg])
        self.load(lb[:], self.ln1_b[li:li + 1, :].to_broadcast([128, D]), [], [lb])
        def mk():
            d_ = {}
            d_["tm"] = self.ln_tmps()
            d_["tm"]["xn"] = self.sb("xn", [128, D], F32)
            d_["in"] = [(self.sb("yTs", [128, 16, 128], BF16), self.sb("gl", [128, GATE_IN], BF16),
                         self.sb("res", [128, D], F32), self.sb("ys", [128, SSD_INNER], BF16)) for _ in range(2)]
            d_["gate"] = self.sb("gate", [128, GATE_IN], F32)
            d_["mx"] = self.sb("mx", [128, D], F32)
            d_["tmp"] = self.sb("mtmp", [128, 512], F32)
            d_["o"] = self.sb("x1o", [128, D], F32)
            return d_
        TS = [mk(), mk()]
        tiles = list(range(NTILE)) if with_ctx else list(range(CT, NTILE))

        def loads(t, sl, par):
            yT, gl, r, ys = TS[sl]["in"][par]
            r0, r1 = t * 128, (t + 1) * 128
            self.load(yT[:, 0:4, :], self.YT[0:4, :, r0:r1].rearrange("j p n -> p j n"), [], [yT])
            self.load(yT[:, 12:16, :], self.YT[12:16, :, r0:r1].rearrange("j p n -> p j n"), [], [yT])
            self.load(ys[:], self.YS[r0:r1, :], [], [ys])
            self.load(gl[:], self.P[r0:r1, O_GATE:IN_W], [], [gl])
            self.load(r[:], src[r0:r1, :], [], [r])

        def body(t, sl, par):
            d_ = TS[sl]
            yT, gl, r, ys = d_["in"][par]
            o, gate, mx, tmp, tm = (d_[k] for k in ("o", "gate", "mx", "tmp", "tm"))
            r0, r1 = t * 128, (t + 1) * 128
            w = 1 if t < CT else 0
            b = self.bank()
            bv = b[:].bitcast(BF16)
            for j in range(8):
                self.transp(b, bv[:, j * 128:(j + 1) * 128], ys[:, j * 128:(j + 1) * 128], self.ident[:],
                            [ys, self.ident])
            self.copy("act", yT[:, 4:12, :], bv[:, 0:1024].rearrange("p (j n) -> p j n", j=8), [b], [yT, b])
            self.act(gate[:], gl[:], AF.Sigmoid, [gl], [gate])
            for br, (k0, k1) in enumerate(((0, 4), (4, 12), (12, 16))):
                for cb in range(2):
                    b = self.bank()
                    for kc in range(k0, k1):
                        self.mm(b, b[:, 0:512], yT[:, kc, :], wout[:, kc, cb * 512:(cb + 1) * 512], kc == k0,
                                kc == k1 - 1, [yT, woutk[kc // 2]])
                    gs = gate[:, br * D + cb * 512:br * D + (cb + 1) * 512]
                    if br == 0:
                        self.tt("dve", mx[:, cb * 512:(cb + 1) * 512], gs, b[:, 0:512], ALU.mult, [gate, b], [mx, b])
                    else:
                        self.tt("dve", tmp[:], gs, b[:, 0:512], ALU.mult, [gate, b], [tmp, b])
                        self.tt("pool", mx[:, cb * 512:(cb + 1) * 512], mx[:, cb * 512:(cb + 1) * 512], tmp[:],
                                ALU.add, [mx, tmp], [mx])
            self.tt("dve", mx[:], mx[:], g1[w][:], ALU.mult, [mx, g1[w]], [mx])
            self.op("dve", lambda e, r=r, mx=mx: e.scalar_tensor_tensor(
                out=mx[:], in0=r[:], scalar=float(self.alpha), in1=mx[:], op0=ALU.mult, op1=ALU.add),
                [r, mx], [mx])
            self.ln_affine_store(mx, tm, lg, lb, o, self.X1[r0:r1, :])
        self.run_streams(tiles, body, 2, loads=loads)
        self.phase_end()

    def ph_ffn(self, li, with_ctx, last):
        NT, NTILE, CT = self.NT, self.NTILE, self.CT_TILES
        NJ = FFN_H // 128
        blocks = []
        if with_ctx:
            blocks.append((0, CT, 1))
        t = CT
        while t < NTILE:
            blocks.append((t, min(t + 4, NTILE), 0))
            t += 4
        self.phase_begin()
        wfi = self.sb("wfi", [128, 8, 2 * FFN_H], BF16)
        wfik = [Trk("wfik%d" % i) for i in range(2 * FFN_H // 512)]
        for i in range(2 * FFN_H // 512):
            self.dma("pool", wfi[:, :, i * 512:(i + 1) * 512],
                     self.w_ffn_in[li, :, i * 512:(i + 1) * 512].rearrange("(kc p) n -> p kc n", p=128),
                     [], [wfik[i]])
        TM = []
        for _ in range(2):
            tm = self.ln_tmps()
            tm["xn"] = self.sb("xn", [128, D], F32)
            TM.append(tm)
        xts = [self.sb("xt", [128, D], F32) for _ in range(2)]
        hns = [self.sb("hn", [128, D], BF16) for _ in range(2)]
        h2Ts = [self.sb("h2T", [128, 8, 512], BF16) for _ in range(2)]
        h2Tk = [[Trk("h2Tk%d_%d" % (a, i)) for i in range(4)] for a in range(2)]
        hids = [self.sb("hid", [128, NJ, 512], BF16) for _ in range(1)] * 2
        sg = [self.sb("sg", [128, 512], F32) for _ in range(2)]
        mods = {}
        for w_ in sorted(set(b_[2] for b_ in blocks)):
            mods[w_] = (self.sb("shb%d" % w_, [128, D], F32), self.sb("scb%d" % w_, [128, D], F32))
            self.mod_rows(mods[w_][0], w_, 3 * D)
            self.mod_rows(mods[w_][1], w_, 4 * D)

        def pro_block(bi_):
            tb0, tb1, w = blocks[bi_]
            nt = tb1 - tb0
            h2T, hk = h2Ts[bi_ % 2], h2Tk[bi_ % 2]
            shb_, scb_ = mods[w]

            def pro(i, sl):
                tt_ = tb0 + i
                tm, hn, xt = TM[sl], hns[sl], xts[sl]
                self.load(xt[:], self.X1[tt_ * 128:(tt_ + 1) * 128, :], [], [xt])
                xn = tm["xn"]
                self.layer_norm_tile(xt[:], xt, D, xn[:], xn, tm)
                self.tt("dve", xn[:], xn[:], scb_[:], ALU.mult, [xn, scb_], [xn])
                self.tt("pool", hn[:], xn[:], shb_[:], ALU.add, [xn, shb_], [hn])
                b = self.bank()
                bv = b[:].bitcast(BF16)
                for kc in range(8):
                    self.transp(b, bv[:, kc * 128:(kc + 1) * 128], hn[:, kc * 128:(kc + 1) * 128], self.ident[:],
                                [hn, self.ident])
                self.copy("act", h2T[:, :, i * 128:(i + 1) * 128], bv[:, 0:1024].rearrange("p (a b) -> p a b", a=8),
                          [b], [hk[i], b])
            self.run_streams(range(nt), pro, 2)

        def j_block(bi_):
            tb0, tb1, w = blocks[bi_]
            nt = tb1 - tb0
            n = nt * 128
            h2T, hk, hid = h2Ts[bi_ % 2], h2Tk[bi_ % 2], hids[bi_ % 2]
            for j in range(NJ):
                bg_, bu_ = self.bank(), self.bank()
                for kc in range(8):
                    self.mm(bg_, bg_[:, 0:n], wfi[:, kc, j * 128:(j + 1) * 128], h2T[:, kc, 0:n], kc == 0, kc == 7,
                            [wfik[(j * 128) // 512]] + hk[0:nt])
                for kc in range(8):
                    self.mm(bu_, bu_[:, 0:n], wfi[:, kc, FFN_H + j * 128:FFN_H + (j + 1) * 128], h2T[:, kc, 0:n],
                            kc == 0, kc == 7, [wfik[(FFN_H + j * 128) // 512]] + hk[0:nt])
                s_ = sg[j % 2]
                self.act(s_[:, 0:n], bg_[:, 0:n], AF.Silu, [bg_], [s_, bg_])
                self.tt("dve", hid[:, j, 0:n], s_[:, 0:n], bu_[:, 0:n], ALU.mult, [s_, bu_], [hid, bu_])
            r0 = tb0 * 128
            half = NJ // 2
            self.store(self.HID[0:half, :, r0:r0 + n].rearrange("j p n -> p j n"), hid[:, 0:half, 0:n], [hid], [])
            self.store(self.HID[half:NJ, :, r0:r0 + n].rearrange("j p n -> p j n"), hid[:, half:NJ, 0:n], [hid], [])

        self.merge_emit([self.record(lambda: pro_block(0), [4, 5])])
        for bi_ in range(len(blocks)):
            lj = self.record(lambda: j_block(bi_), [0, 1, 2, 3])
            lp = self.record(lambda: pro_block(bi_ + 1), [4, 5]) if bi_ + 1 < len(blocks) else []
            self.merge_emit([lj, lp] if lp else [lj])
        self.phase_end()
        self.phase_begin()
        wfo = self.sb("wfo", [128, NJ, D], BF16)
        wfok = [Trk("wfok%d" % i) for i in range(NJ // 2)]
        for i in range(NJ // 2):
            self.dma("pool", wfo[:, 2 * i:2 * i + 2, :],
                     self.w_ffn_out[li, i * 256:(i + 1) * 256, :].rearrange("(kc p) n -> p kc n", p=128),
                     [], [wfok[i]])
        lg = self.sb("lg", [128, D], F32)
        lb = self.sb("lb", [128, D], F32)
        self.load(lg[:], self.ln2_g[li:li + 1, :].to_broadcast([128, D]), [], [lg])
        self.load(lb[:], self.ln2_b[li:li + 1, :].to_broadcast([128, D]), [], [lb])
        g2 = [self.sb("g2b", [128, D], F32) for _ in range(2)]
        for w in range(2):
            self.mod_rows(g2[w], w, 5 * D)

        def mk():
            d_ = {}
            d_["tm"] = self.ln_tmps()
            d_["tm"]["xn"] = self.sb("xn", [128, D], F32)
            d_["in"] = [(self.sb("hd", [128, NJ, 128], BF16), self.sb("x1t", [128, D], F32)) for _ in range(2)]
            d_["pre"] = self.sb("pre", [128, D], F32)
            d_["o"] = self.sb("x2o", [128, D], F32)
            return d_
        TS = [mk(), mk(), mk()]
        tiles = list(range(NTILE)) if with_ctx else list(range(CT, NTILE))

        def eloads(tt_, sl, par):
            hd, x1 = TS[sl]["in"][par]
            r0, r1 = tt_ * 128, (tt_ + 1) * 128
            half = NJ // 2
            self.load(hd[:, 0:half, :], self.HID[0:half, :, r0:r1].rearrange("j p n -> p j n"), [], [hd])
            self.load(hd[:, half:NJ, :], self.HID[half:NJ, :, r0:r1].rearrange("j p n -> p j n"), [], [hd])
            self.load(x1[:], self.X1[r0:r1, :], [], [x1])

        def epi(tt_, sl, par):
            d_ = TS[sl]
            tm, pre, o = d_["tm"], d_["pre"], d_["o"]
            hd, x1 = d_["in"][par]
            w = 1 if tt_ < CT else 0
            r0, r1 = tt_ * 128, (tt_ + 1) * 128
            for cb in range(2):
                b = self.bank()
                for j in range(NJ):
                    self.mm(b, b[:, 0:512], hd[:, j, :], wfo[:, j, cb * 512:(cb + 1) * 512], j == 0, j == NJ - 1,
                            [hd, wfok[j // 2]])
                self.tt("dve", pre[:, cb * 512:(cb + 1) * 512], g2[w][:, cb * 512:(cb + 1) * 512], b[:, 0:512],
                        ALU.mult, [g2[w], b], [pre, b])
            self.op("dve", lambda e, x1=x1, pre=pre: e.scalar_tensor_tensor(
                out=pre[:], in0=x1[:], scalar=float(self.alpha), in1=pre[:], op0=ALU.mult, op1=ALU.add),
                [x1, pre], [pre])
            if last:
                dst = self.out[(tt_ - CT) * 128:(tt_ - CT + 1) * 128, :]
            else:
                dst = self.X2[r0:r1, :]
            self.ln_affine_store(pre, tm, lg, lb, o, dst)
        self.run_streams(tiles, epi, 3, loads=eloads)
        self.phase_end()

    def build(self, upto=99):
        self.declare()
        self.stack = ExitStack()
        self.setup_consts()
        self.stack = None
        for li in range(self.DEPTH):
            last = li == self.DEPTH - 1
            with_ctx = not last
            src = self.x_in if li == 0 else self.X2
            self.mod_li = li
            if li == 0:
                self.ph_mod(li)
            self.ph_proj(li, src, None)
            self.ph_post(li)
            for d_ in range(2):
                self.phase_begin()
                if d_ == 0:
                    fa = self.attn_setup(li, self.QTm, self.KTm, self.Vm, MLA_H, lambda h: h, MLA_QK,
                                         MLA_QK ** -0.5, 0, with_ctx)
                else:
                    fa = self.attn_setup(li, self.QTg, self.KTg, self.Vg, GQA_H, lambda h: h // 4, GQA_D,
                                         GQA_D ** -0.5, 12, with_ctx)
                fb = self.ssd_setup(li, d_)
                la = self.record(fa, [0, 1, 2])
                lb = self.record(fb, [4, 5])
                self.merge_emit([la, lb])
                self.phase_end()
            self.ph_merge(li, src, with_ctx)
            self.ph_ffn(li, with_ctx, last)
        self.S.finalize()
        return self.nc


def rope_tables(seq, ctx, rot_dim):
    rows = seq // GRID_W
    row = np.repeat(np.arange(rows), GRID_W).astype(np.float32)
    col = np.tile(np.arange(GRID_W), rows).astype(np.float32)
    n_freq = rot_dim // 4
    inv_freq = (np.float32(10000.0) ** (-np.arange(n_freq, dtype=np.float32) / np.float32(n_freq))).astype(np.float32)
    ang = np.concatenate([row[:, None] * inv_freq, col[:, None] * inv_freq], axis=-1).astype(np.float32)
    tab = np.zeros((ctx + seq, 2, rot_dim // 2), np.float32)
    tab[:ctx, 0, :] = 1.0
    tab[ctx:, 0, :] = np.cos(ang)
    tab[ctx:, 1, :] = np.sin(ang)
    return tab


_CACHE = {}


def make_in_maps(inp, n_cores, SEQ, CTX, L):
    f = lambda a: np.ascontiguousarray(np.asarray(a, dtype=np.float32))
    B = inp["x"].shape[0]
    shared = {k: f(inp[k]) for k in ("w_mod", "b_mod", "w_in", "b_gate", "w_uq", "g_q_mla", "w_ukv", "g_kv_mla",
                                     "d_skip", "g_ssd", "g_q_gqa", "g_k_gqa", "w_out", "ln1_g", "ln1_b",
                                     "w_ffn_in", "w_ffn_out", "ln2_g", "ln2_b")}
    shared["a_log"] = f(inp["a_log"]).reshape(L, 32)
    shared["dt_bias"] = f(inp["dt_bias"]).reshape(L, 32)
    cw = f(inp["conv_w"])
    shared["conv_w_l"] = np.ascontiguousarray(cw.reshape(L, 3, 12, 128).transpose(0, 3, 2, 1))
    shared["conv_b_l"] = np.ascontiguousarray(f(inp["conv_b"]).reshape(L, 12, 128).transpose(0, 2, 1))
    shared["rope_m"] = rope_tables(SEQ, CTX, MLA_ROPE)
    shared["rope_g"] = rope_tables(SEQ, CTX, GQA_D)
    maps = []
    for c in range(n_cores):
        b = c % B
        m = dict(shared)
        m["x_in"] = np.ascontiguousarray(np.concatenate([f(inp["ctx"][b]), f(inp["x"][b])], axis=0))
        cl = np.stack([f(inp["c"][b]).reshape(8, 128).T, f(inp["c_ctx"]).reshape(8, 128).T], axis=-1)
        m["c_lay"] = np.ascontiguousarray(cl)
        maps.append(m)
    return maps


def kernel(**inputs):
    x = np.asarray(inputs["x"])
    B, SEQ, _ = x.shape
    CTX = np.asarray(inputs["ctx"]).shape[1]
    L = np.asarray(inputs["w_in"]).shape[0]
    alpha = (2 * L) ** 0.25
    n_cores = 8
    key = (SEQ, CTX, L)
    if key not in _CACHE:
        _CACHE[key] = Model(SEQ, CTX, L, alpha).build()
    nc = _CACHE[key]
    in_maps = make_in_maps(inputs, n_cores, SEQ, CTX, L)
    res = run_bass_kernel_spmd(nc, in_maps, core_ids=list(range(n_cores)))
    out = np.stack([np.asarray(res.results[b]["out"], dtype=np.float32) for b in range(B)], axis=0)
    return out
```
